# Optimizing a Trainium2 kernel written in Bass

```python
import math
import jax, jax.numpy as jnp
from jax import lax
import numpy as np

D_MODEL = 2048
BATCH = 2
SEQ = 16384
DEPTH = 2
DEC_BATCH = 8
DEC_SEQ = 64
PAST_LEN = 4096

CHUNK = 64
N_META = 16
N_A = DEPTH // 2
N_B = DEPTH - N_A
D_FF = 5632
CONV_W = 31
N_HEADS = 32
HEAD_DIM = 64
KV_HEADS = 4
GROUP = N_HEADS // KV_HEADS
WINDOW = 128
W_CHUNKS = -(-WINDOW // CHUNK)
WIN_ROWS = W_CHUNKS * CHUNK
BAND = (W_CHUNKS + 1) * CHUNK
N_BUCKETS = 32
MAX_DISTANCE = 128
EPS = 1e-6
SCALE = HEAD_DIM ** -0.5
NEG_INF = -1e30

kernel_name = "yoco_conformer_swa_sink_stream_step"


def rms_norm(x, g):
    xf = x.astype(jnp.float32)
    y = xf * lax.rsqrt(jnp.mean(xf * xf, axis=-1, keepdims=True) + EPS)
    return (y * g.astype(jnp.float32)).astype(x.dtype)


def layer_norm(x, g, b):
    xf = x.astype(jnp.float32)
    xc = xf - jnp.mean(xf, axis=-1, keepdims=True)
    y = xc * lax.rsqrt(jnp.mean(xc * xc, axis=-1, keepdims=True) + EPS)
    return (y * g.astype(jnp.float32) + b.astype(jnp.float32)).astype(x.dtype)


def half_ffn(x, g, w_gate, w_up, w_down):
    h = rms_norm(x, g)
    return 0.5 * ((jax.nn.silu(h @ w_gate) * (h @ w_up)) @ w_down)


def conv_module(x, buf, g, w_pw1, b_pw1, w_dw, b_dw, ln_g, ln_b, w_pw2, b_pw2):
    h = rms_norm(x, g)
    a, gate = jnp.split(h @ w_pw1 + b_pw1, 2, axis=-1)
    u = a * jax.nn.sigmoid(gate)
    up = jnp.concatenate([buf.astype(u.dtype), u], axis=1)
    y = lax.conv_general_dilated(
        up, w_dw.astype(u.dtype)[:, None, :], window_strides=(1,), padding="VALID",
        dimension_numbers=("NWC", "WIO", "NWC"), feature_group_count=D_MODEL) + b_dw
    y = jax.nn.silu(layer_norm(y, ln_g, ln_b))
    return y @ w_pw2 + b_pw2, up[:, up.shape[1] - (CONV_W - 1):]


def shared_kv(x, kv_norm, w_k, w_v, k_norm):
    n, t = x.shape[:2]
    h = rms_norm(x, kv_norm)
    k = rms_norm((h @ w_k).reshape(n, t, KV_HEADS, HEAD_DIM), k_norm)
    v = (h @ w_v).reshape(n, t, KV_HEADS, HEAD_DIM)
    return k, v


def t5_bias(table, rel):
    nb = N_BUCKETS // 2
    max_exact = nb // 2
    n = jnp.abs(rel)
    large = max_exact + (jnp.log(jnp.maximum(n, 1).astype(jnp.float32) / max_exact)
                         / math.log(MAX_DISTANCE / max_exact) * (nb - max_exact)).astype(jnp.int32)
    large = jnp.minimum(large, nb - 1)
    bucket = jnp.where(rel > 0, nb, 0) + jnp.where(n < max_exact, n, large)
    b = jnp.moveaxis(table[bucket].astype(jnp.float32), -1, -3)
    return b.reshape(b.shape[:-3] + (KV_HEADS, GROUP) + b.shape[-2:])


def sink_softmax(logits_list, sink):
    lead = logits_list[0].shape[:-1]
    col = jnp.broadcast_to(sink.astype(jnp.float32).reshape(KV_HEADS, GROUP, 1, 1), lead + (1,))
    p = jax.nn.softmax(jnp.concatenate(logits_list + [col], axis=-1), axis=-1)
    bounds = [int(b) for b in np.cumsum([l.shape[-1] for l in logits_list])]
    return jnp.split(p, bounds, axis=-1)[:-1]


def window_attention_prompt(q, k, v, k_meta, v_meta, sink, table):
    n, t = q.shape[:2]
    nc = t // CHUNK
    qg = q.reshape(n, nc, CHUNK, KV_HEADS, GROUP, HEAD_DIM)

    def band(a):
        pad = jnp.zeros((n, W_CHUNKS * CHUNK) + a.shape[2:], a.dtype)
        ap = jnp.concatenate([pad, a], axis=1).reshape((n, nc + W_CHUNKS, CHUNK) + a.shape[2:])
        return jnp.concatenate([ap[:, j:j + nc] for j in range(W_CHUNKS + 1)], axis=2)

    kb, vb = band(k), band(v)
    i = jnp.arange(CHUNK)
    s = jnp.arange(BAND)
    c = jnp.arange(nc)
    m = jnp.arange(N_META)
    bias_band = t5_bias(table, s[None, :] - W_CHUNKS * CHUNK - i[:, None])
    q_pos = N_META + c[:, None] * CHUNK + i[None, :]
    bias_meta = t5_bias(table, m[None, None, :] - q_pos[:, :, None])
    valid = (c[:, None] - W_CHUNKS + s[None, :] // CHUNK >= 0)[:, None, None, None, :]
    lm = jnp.einsum("ncqkgd,nskd->nckgqs", qg, k_meta).astype(jnp.float32) + bias_meta
    lb = jnp.einsum("ncqkgd,ncskd->nckgqs", qg, kb).astype(jnp.float32) + bias_band
    lb = jnp.where(valid, lb, NEG_INF)
    pm, pb = sink_softmax([lm, lb], sink)
    o = (jnp.einsum("nckgqs,nskd->ncqkgd", pm.astype(v.dtype), v_meta)
         + jnp.einsum("nckgqs,ncskd->ncqkgd", pb.astype(v.dtype), vb))
    return o.reshape(n, t, N_HEADS * HEAD_DIM)


def window_attention_sample(q, k, v, cache_meta_k, cache_meta_v, cache_win_k, cache_win_v, sink, table):
    n, s_new = q.shape[:2]
    win = cache_win_k.shape[1]
    qg = q.reshape(n, s_new, KV_HEADS, GROUP, HEAD_DIM)
    kc = jnp.concatenate([cache_meta_k.astype(k.dtype), cache_win_k.astype(k.dtype), k], axis=1)
    vc = jnp.concatenate([cache_meta_v.astype(v.dtype), cache_win_v.astype(v.dtype), v], axis=1)
    q_pos = N_META + PAST_LEN + jnp.arange(s_new)
    k_pos = jnp.concatenate([jnp.arange(N_META), N_META + PAST_LEN - win + jnp.arange(win), q_pos])
    bias = t5_bias(table, k_pos[None, :] - q_pos[:, None])
    logits = jnp.einsum("nqkgd,nskd->nkgqs", qg, kc).astype(jnp.float32) + bias
    (p,) = sink_softmax([logits], sink)
    o = jnp.einsum("nkgqs,nskd->nqkgd", p.astype(vc.dtype), vc)
    return o.reshape(n, s_new, N_HEADS * HEAD_DIM)


def trunk(x, conv_bufs, n_lead, attend, ffn_p, conv_p, kv_p, attn_p):
    ffn_norm, ffn_w_gate, ffn_w_up, ffn_w_down = ffn_p
    conv_norm, w_pw1, b_pw1, w_dw, b_dw, ln_g, ln_b, w_pw2, b_pw2 = conv_p
    kv_norm, w_k, w_v, k_norm = kv_p
    attn_norm, w_q, q_norm, w_o = attn_p
    new_bufs = []
    k = v = k_lead = v_lead = None
    for i in range(DEPTH):
        if i == N_A:
            k_all, v_all = shared_kv(x, kv_norm, w_k, w_v, k_norm)
            k_lead, v_lead = k_all[:, :n_lead], v_all[:, :n_lead]
            k, v, x = k_all[:, n_lead:], v_all[:, n_lead:], x[:, n_lead:]
        x = x + half_ffn(x, ffn_norm[i, 0], ffn_w_gate[i, 0], ffn_w_up[i, 0], ffn_w_down[i, 0])
        if i < N_A:
            y, nb = conv_module(x, conv_bufs[i], conv_norm[i], w_pw1[i], b_pw1[i], w_dw[i], b_dw[i],
                                ln_g[i], ln_b[i], w_pw2[i], b_pw2[i])
            new_bufs.append(nb)
        else:
            l = i - N_A
            h = rms_norm(x, attn_norm[l])
            q = rms_norm((h @ w_q[l]).reshape(x.shape[:2] + (N_HEADS, HEAD_DIM)), q_norm[l]) * SCALE
            y = attend(q, k, v, k_lead, v_lead, l) @ w_o[l]
        x = x + y
        x = x + half_ffn(x, ffn_norm[i, 1], ffn_w_gate[i, 1], ffn_w_up[i, 1], ffn_w_down[i, 1])
    return x, jnp.stack(new_bufs), k, v, k_lead, v_lead


def setup_inputs(seed: int = 0) -> dict:
    key = jax.random.key(seed)
    ks = jax.random.split(key, 40)

    def nrm(k, shape, scale):
        return jax.random.normal(k, shape, jnp.float32) * scale

    win_rows = min(WIN_ROWS, PAST_LEN)
    return {
        "x_prompt": nrm(ks[0], (BATCH, SEQ, D_MODEL), 1.0),
        "x_sample": nrm(ks[1], (DEC_BATCH, DEC_SEQ, D_MODEL), 1.0),
        "state_conv": nrm(ks[2], (N_A, DEC_BATCH, CONV_W - 1, D_MODEL), 0.5),
        "cache_meta_k": nrm(ks[3], (DEC_BATCH, N_META, KV_HEADS, HEAD_DIM), 1.0),
        "cache_meta_v": nrm(ks[4], (DEC_BATCH, N_META, KV_HEADS, HEAD_DIM), 1.0),
        "cache_win_k": nrm(ks[5], (DEC_BATCH, win_rows, KV_HEADS, HEAD_DIM), 1.0),
        "cache_win_v": nrm(ks[6], (DEC_BATCH, win_rows, KV_HEADS, HEAD_DIM), 1.0),
        "meta_tokens": nrm(ks[7], (N_META, D_MODEL), 1.0),
        "ffn_norm": 1.0 + nrm(ks[8], (DEPTH, 2, D_MODEL), 0.05),
        "ffn_w_gate": nrm(ks[9], (DEPTH, 2, D_MODEL, D_FF), D_MODEL ** -0.5),
        "ffn_w_up": nrm(ks[10], (DEPTH, 2, D_MODEL, D_FF), D_MODEL ** -0.5),
        "ffn_w_down": nrm(ks[11], (DEPTH, 2, D_FF, D_MODEL), D_FF ** -0.5),
        "conv_norm": 1.0 + nrm(ks[12], (N_A, D_MODEL), 0.05),
        "conv_w_pw1": nrm(ks[13], (N_A, D_MODEL, 2 * D_MODEL), D_MODEL ** -0.5),
        "conv_b_pw1": nrm(ks[14], (N_A, 2 * D_MODEL), 0.02),
        "conv_w_dw": nrm(ks[15], (N_A, CONV_W, D_MODEL), CONV_W ** -0.5),
        "conv_b_dw": nrm(ks[16], (N_A, D_MODEL), 0.02),
        "conv_ln_g": 1.0 + nrm(ks[17], (N_A, D_MODEL), 0.05),
        "conv_ln_b": nrm(ks[18], (N_A, D_MODEL), 0.02),
        "conv_w_pw2": nrm(ks[19], (N_A, D_MODEL, D_MODEL), D_MODEL ** -0.5),
        "conv_b_pw2": nrm(ks[20], (N_A, D_MODEL), 0.02),
        "kv_norm": 1.0 + nrm(ks[21], (D_MODEL,), 0.05),
        "w_k": nrm(ks[22], (D_MODEL, KV_HEADS * HEAD_DIM), D_MODEL ** -0.5),
        "w_v": nrm(ks[23], (D_MODEL, KV_HEADS * HEAD_DIM), D_MODEL ** -0.5),
        "k_norm": 1.0 + nrm(ks[24], (HEAD_DIM,), 0.05),
        "attn_norm": 1.0 + nrm(ks[25], (N_B, D_MODEL), 0.05),
        "w_q": nrm(ks[26], (N_B, D_MODEL, N_HEADS * HEAD_DIM), D_MODEL ** -0.5),
        "q_norm": 1.0 + nrm(ks[27], (N_B, HEAD_DIM), 0.05),
        "sinks": nrm(ks[28], (N_B, N_HEADS), 0.5),
        "w_o": nrm(ks[29], (N_B, N_HEADS * HEAD_DIM, D_MODEL), (N_HEADS * HEAD_DIM) ** -0.5),
        "rel_bias_table": nrm(ks[30], (N_BUCKETS, N_HEADS), 0.5),
    }


def reference(x_prompt, x_sample, state_conv, cache_meta_k, cache_meta_v, cache_win_k, cache_win_v,
              meta_tokens, ffn_norm, ffn_w_gate, ffn_w_up, ffn_w_down,
              conv_norm, conv_w_pw1, conv_b_pw1, conv_w_dw, conv_b_dw, conv_ln_g, conv_ln_b,
              conv_w_pw2, conv_b_pw2, kv_norm, w_k, w_v, k_norm,
              attn_norm, w_q, q_norm, sinks, w_o, rel_bias_table):
    ffn_p = (ffn_norm, ffn_w_gate, ffn_w_up, ffn_w_down)
    conv_p = (conv_norm, conv_w_pw1, conv_b_pw1, conv_w_dw, conv_b_dw, conv_ln_g, conv_ln_b,
              conv_w_pw2, conv_b_pw2)
    kv_p = (kv_norm, w_k, w_v, k_norm)
    attn_p = (attn_norm, w_q, q_norm, w_o)

    n_p = x_prompt.shape[0]
    meta = jnp.broadcast_to(meta_tokens.astype(x_prompt.dtype)[None], (n_p, N_META, D_MODEL))
    x0 = jnp.concatenate([meta, x_prompt], axis=1)
    zero_bufs = jnp.zeros((N_A, n_p, CONV_W - 1, D_MODEL), x_prompt.dtype)

    def attend_prompt(q, k, v, k_lead, v_lead, l):
        return window_attention_prompt(q, k, v, k_lead, v_lead, sinks[l], rel_bias_table)

    y_prompt, p_conv, p_k, p_v, p_meta_k, p_meta_v = trunk(
        x0, zero_bufs, N_META, attend_prompt, ffn_p, conv_p, kv_p, attn_p)
    p_win_k = p_k[:, p_k.shape[1] - WIN_ROWS:]
    p_win_v = p_v[:, p_v.shape[1] - WIN_ROWS:]

    def attend_sample(q, k, v, k_lead, v_lead, l):
        return window_attention_sample(q, k, v, cache_meta_k, cache_meta_v, cache_win_k, cache_win_v,
                                       sinks[l], rel_bias_table)

    y_sample, s_conv, s_k, s_v, _, _ = trunk(
        x_sample, state_conv, 0, attend_sample, ffn_p, conv_p, kv_p, attn_p)

    return (y_prompt, y_sample, p_conv, p_meta_k, p_meta_v, p_win_k, p_win_v, s_conv, s_k, s_v)
```

```python
import contextlib
import math
import numpy as np
import concourse.bass as bass
import concourse.mybir as mybir
from concourse.bass_utils import run_bass_kernel_spmd

F32 = mybir.dt.float32
BF16 = mybir.dt.bfloat16
AF = mybir.ActivationFunctionType
ALU = mybir.AluOpType

P = 128
D = 2048
KD = 16
DFF = 5632
KF = 44
TN = 512
NS = 268
HALO = 188
META0 = 188
SMP0 = 204
EPS = 1e-6
SLOT = 8192
NRING = 3
SCRW = 19968
KW = 17 + 128 + TN
KWS = 17 + 128 + 64


class Clock:
    def __init__(self, sem, name):
        self.sem = sem; self.cnt = 0; self.name = name


class Eng:
    def __init__(self, e, clock, name, self_sync=True):
        self.e = e; self.clock = clock; self.name = name; self.seen = {}; self.self_sync = self_sync


class FW:
    def __init__(self, nc, stack, n_dma_sems=32):
        self.nc = nc

        def mk(name):
            return Clock(stack.enter_context(nc.semaphore(name)), name)
        self.pe = Eng(nc.tensor, mk("c_pe"), "pe", self_sync=False)
        self.act = Eng(nc.scalar, mk("c_act"), "act")
        self.dve = Eng(nc.vector, mk("c_dve"), "dve")
        self.pool = Eng(nc.gpsimd, mk("c_pool"), "pool")
        self.sp = Eng(nc.sync, mk("c_sp"), "sp")
        self.dma_clocks = {"sp": [mk(f"c_dmas{i}") for i in range(n_dma_sems // 2)],
                           "pool": [mk(f"c_dmap{i}") for i in range(n_dma_sems // 2)]}
        self.dma_rr = {"sp": 0, "pool": 0}
        self.last_w = {}
        self.readers = {}
        self.fence = []
        self.nwaits = 0
        self.nops = 0

    def _wait(self, eng, stamp):
        clock, val = stamp
        if clock is eng.clock and not eng.self_sync:
            return
        if eng.seen.get(clock.name, 0) >= val:
            return
        eng.e.wait_ge(clock.sem, val)
        eng.seen[clock.name] = val
        self.nwaits += 1

    def _deps(self, eng, reads, writes):
        for k in reads:
            s = self.last_w.get(k)
            if s is not None:
                self._wait(eng, s)
        for k in writes:
            s = self.last_w.get(k)
            if s is not None:
                self._wait(eng, s)
            rs = self.readers.get(k)
            if rs:
                for r in rs:
                    self._wait(eng, r)
            if s is None and rs is None and isinstance(k, tuple) and k[0] == "s":
                for f in self.fence:
                    self._wait(eng, f)

    def _commit(self, stamp, reads, writes):
        for k in writes:
            self.last_w[k] = stamp
            self.readers[k] = []
        for k in reads:
            lst = self.readers.setdefault(k, [])
            lst.append(stamp)
            if len(lst) > 48:
                best = {}
                for c, v in lst:
                    if c.name not in best or best[c.name][1] < v:
                        best[c.name] = (c, v)
                self.readers[k] = list(best.values())

    def op(self, eng, fn, reads=(), writes=()):
        self._deps(eng, reads, writes)
        ins = fn()
        self.nops += 1
        eng.clock.cnt += 1
        ins.then_inc(eng.clock.sem, 1)
        self._commit((eng.clock, eng.clock.cnt), reads, writes)

    def pe_group(self, fns, reads=(), writes=()):
        eng = self.pe
        self._deps(eng, reads, writes)
        ins = None
        for fn in fns:
            ins = fn()
        self.nops += len(fns)
        eng.clock.cnt += 1
        ins.then_inc(eng.clock.sem, 1)
        self._commit((eng.clock, eng.clock.cnt), reads, writes)

    def dma(self, q, out, in_, reads=(), writes=()):
        cl = self.dma_clocks[q.name]
        clock = cl[self.dma_rr[q.name]]
        self.dma_rr[q.name] = (self.dma_rr[q.name] + 1) % len(cl)
        if clock.cnt > 0:
            self._wait(q, (clock, clock.cnt))
        self._deps(q, reads, writes)
        ins = q.e.dma_start(out=out, in_=in_)
        clock.cnt += 16
        ins.then_inc(clock.sem, 16)
        self._commit((clock, clock.cnt), reads, writes)
        self.nops += 1

    def split_key(self, whole, subs):
        for sub in subs:
            if whole in self.last_w:
                self.last_w[sub] = self.last_w[whole]
            self.readers[sub] = list(self.readers.get(whole, []))

    def merge_key(self, whole, subs):
        allst = list(self.readers.get(whole, []))
        for sub in subs:
            s_ = self.last_w.pop(sub, None)
            if s_ is not None:
                allst.append(s_)
            allst += self.readers.pop(sub, [])
        best = {}
        for c, v in allst:
            if c.name not in best or best[c.name][1] < v:
                best[c.name] = (c, v)
        self.readers[whole] = list(best.values())

    def fence_scratch(self):
        best = {}
        for f in self.fence:
            best[f[0].name] = f
        dead = [k for k in list(self.last_w.keys()) + list(self.readers.keys())
                if isinstance(k, tuple) and k[0] == "s"]
        for k in set(dead):
            stamps = []
            s = self.last_w.pop(k, None)
            if s is not None:
                stamps.append(s)
            stamps += self.readers.pop(k, [])
            for c, v in stamps:
                if c.name not in best or best[c.name][1] < v:
                    best[c.name] = (c, v)
        self.fence = list(best.values())

    def wait_all(self, eng):
        for k, s in list(self.last_w.items()):
            self._wait(eng, s)
        for k, rs in list(self.readers.items()):
            for r in rs:
                self._wait(eng, r)


def build_program(NT):
    nc = bass.Bass("TRN2", target_bir_lowering=False)

    def din(name, shape):
        return nc.dram_tensor(name, list(shape), F32, kind="ExternalInput").ap()

    def dout(name, shape):
        return nc.dram_tensor(name, list(shape), F32, kind="ExternalOutput").ap()

    LM = NT * TN
    xm = din("xm", [LM, D]); xs = din("xs", [NS, D]); umask_d = din("umask", [P, NS])
    sconv_d = din("sconv", [30, D]); ckd_d = din("ckd", [144, 512]); cv_d = din("cv", [144, 256])
    w_gate = din("w_gate", [4, D, DFF]); w_up = din("w_up", [4, D, DFF]); w_down = din("w_down", [4, DFF, D])
    w_pw1 = din("w_pw1", [D, 2 * D]); w_pw2 = din("w_pw2", [D, D]); wkd_d = din("wkd", [D, 512])
    wvd_d = din("wvd", [D, 512]); w_q = din("w_q", [D, D]); w_o = din("w_o", [D, D])
    gains_d = din("gains", [P, 11 * 16]); bpw1_d = din("bpw1", [P, 32]); wdw_d = din("wdw", [P, 16 * 31])
    kqn_d = din("kqn", [P, 2]); taug_d = din("taug", [34, 32])
    ohA_d = din("ohA", [3, 34, 64 * 128]); ohC_d = din("ohC", [34, 64 * 64]); ohM_d = din("ohM", [3, 34, 64 * 17])
    ident_d = din("ident", [P, P])

    y_main = dout("y_main", [LM, D]); y_smp = dout("y_smp", [64, D])
    cs_p = dout("cs_p", [30, D]); cs_s = dout("cs_s", [30, D])
    kmeta_o = dout("kmeta", [16, 256]); vmeta_o = dout("vmeta", [16, 256])
    kwin_o = dout("kwin", [128, 256]); vwin_o = dout("vwin", [128, 256])
    knew_o = dout("knew", [64, 256]); vnew_o = dout("vnew", [64, 256])

    biasd = nc.dram_tensor("biasd", [7, P, 2048], F32).ap()

    wst = {}
    for f in range(4):
        wst[("gu", f)] = (22, 8192)
        wst[("dn", f)] = (16, 5632)
    wst["pw1"] = (8, 8192); wst["pw2"] = (8, 4096); wst["wk"] = (1, 8192); wst["wv"] = (1, 8192)
    wst["wq"] = (8, 4096); wst["wo"] = (8, 4096)
    wscr = {}
    for i, (k, (ns_, el)) in enumerate(wst.items()):
        wscr[k] = nc.dram_tensor(f"wscr{i}", [ns_, P, el], BF16).ap()

    def kpc(w):
        return w.rearrange("(k p) c -> p k c", p=P)

    def wsrc(key, n):
        if isinstance(key, tuple) and key[0] == "gu":
            f = key[1]
            return [(0, 4096, (16, 256), kpc(w_gate[f])[:, :, n * 256:(n + 1) * 256]),
                    (4096, 8192, (16, 256), kpc(w_up[f])[:, :, n * 256:(n + 1) * 256])]
        if isinstance(key, tuple) and key[0] == "dn":
            f = key[1]; fp, hh = n // 2, n % 2
            return [(0, 5632, (22, 256), kpc(w_down[f])[:, hh * 22:(hh + 1) * 22, fp * 256:(fp + 1) * 256])]
        if key == "pw1":
            return [(0, 4096, (16, 256), kpc(w_pw1)[:, :, n * 256:(n + 1) * 256]),
                    (4096, 8192, (16, 256), kpc(w_pw1)[:, :, D + n * 256:D + (n + 1) * 256])]
        if key in ("pw2", "wq", "wo"):
            w = {"pw2": w_pw2, "wq": w_q, "wo": w_o}[key]
            return [(0, 4096, (16, 256), kpc(w)[:, :, n * 256:(n + 1) * 256])]
        if key == "wk":
            return [(0, 8192, (16, 512), kpc(wkd_d))]
        if key == "wv":
            return [(0, 8192, (16, 512), kpc(wvd_d))]
        raise KeyError(key)

    with contextlib.ExitStack() as st:
        fw = FW(nc, st)
        pe, act, dve, pool, sp = fw.pe, fw.act, fw.dve, fw.pool, fw.sp

        def sb(name, shape, dt):
            return st.enter_context(nc.sbuf_tensor("s_" + name, list(shape), dt))

        xT = sb("xT", [P, KD, TN], F32)
        hT = sb("hT", [P, KD, TN], BF16)
        wring = sb("wring", [P, NRING, SLOT], BF16)
        scr = sb("scr", [P, SCRW], F32)
        hprev = sb("hprev", [P, KD, 128], BF16)
        KT = sb("KT", [P, 4, KW], BF16)
        KTs = sb("KTs", [P, 4, KWS], BF16)
        Vatt = sb("Vatt", [P, 10, 512], BF16)
        Vmeta = sb("Vmeta", [P, 512], BF16)
        VsA = sb("VsA", [P, 512], BF16); VsC = sb("VsC", [P, 512], BF16); VsM = sb("VsM", [P, 512], BF16)
        hist = sb("hist", [P, KD, 30], F32)
        ident = sb("ident", [P, P], F32)
        identb = sb("identb", [P, P], BF16)
        onesD = sb("onesD", [P, P], BF16)
        bd64 = sb("bd64", [P, P], BF16)
        onesB = sb("onesB", [P, P], BF16)
        gains = sb("gains", [P, 11, 16], F32)
        bpw1 = sb("bpw1", [P, 32], F32)
        wdw = sb("wdw", [P, 16, 31], F32)
        kqn = sb("kqn", [P, 2], F32)
        qg = sb("qg", [P, 1], F32)
        epsT = sb("epsT", [P, 1], F32)
        umask = sb("umask", [P, NS], F32)
        taug = sb("taug", [P, 32], F32)
        ps = [st.enter_context(nc.psum_tensor(f"ps{i}", [P, 512], F32)) for i in range(8)]

        def PSK(b):
            return ("ps", b)

        class Scr:
            def __init__(self):
                self.off = 0

            def reset(self):
                fw.fence_scratch(); self.off = 0

            def f32(self, n):
                a = self.off; self.off += n
                assert self.off <= SCRW, self.off
                return scr[:, a:a + n]

            def bf16(self, n):
                n2 = (n + 1) // 2
                a = self.off; self.off += n2
                assert self.off <= SCRW, self.off
                return scr[:, a:a + n2].bitcast(BF16)[:, 0:n]
        S = Scr()

        cq = sp
        fw.dma(cq, ident[:], ident_d, writes=["ident"])
        fw.dma(cq, gains[:], gains_d.rearrange("p (a b) -> p a b", b=16), writes=["gains"])
        fw.dma(cq, bpw1[:], bpw1_d, writes=["bpw1"])
        fw.dma(cq, wdw[:], wdw_d.rearrange("p (a b) -> p a b", b=31), writes=["wdw"])
        fw.dma(cq, kqn[:], kqn_d, writes=["kqn"])
        fw.dma(cq, umask[:], umask_d, writes=["umask"])
        fw.dma(cq, taug[0:34, :], taug_d, writes=["taug"])
        fw.op(dve, lambda: nc.vector.memset(onesD[:], 1.0 / D), writes=["onesD"])
        fw.op(dve, lambda: nc.vector.tensor_copy(out=identb[:], in_=ident[:]), reads=["ident"], writes=["identb"])
        fw.op(dve, lambda: nc.vector.memset(onesB[:], 1.0), writes=["onesB"])
        fw.op(dve, lambda: nc.vector.memset(bd64[:], 0.0), writes=["bd64"])
        fw.op(dve, lambda: nc.vector.memset(bd64[0:64, 0:64], 1.0 / 64), reads=["bd64"], writes=["bd64"])
        fw.op(dve, lambda: nc.vector.memset(bd64[64:128, 64:128], 1.0 / 64), reads=["bd64"], writes=["bd64"])
        fw.op(dve, lambda: nc.vector.memset(epsT[:], EPS), writes=["epsT"])
        fw.op(dve, lambda: nc.vector.tensor_scalar(out=qg[:], in0=kqn[:, 1:2], scalar1=0.125, scalar2=None,
                                                   op0=ALU.mult), reads=["kqn"], writes=["qg"])
        fw.op(dve, lambda: nc.vector.memset(KT[:], 0.0), writes=["KT"])
        fw.op(dve, lambda: nc.vector.memset(KTs[:], 0.0), writes=["KTs"])
        fw.op(dve, lambda: nc.vector.memset(Vmeta[:], 0.0), writes=["Vmeta"])
        fw.op(dve, lambda: nc.vector.memset(VsM[:], 0.0), writes=["VsM"])
        fw.op(dve, lambda: nc.vector.memset(hist[:], 0.0), writes=["hist"])
        fw.op(dve, lambda: nc.vector.memset(hprev[:], 0.0), writes=["hprev"])

        def build_bias():
            S.reset()
            ohb = S.f32(8192)
            bt = S.f32(2048)
            variants = [(0, ohA_d[0], 128), (1, ohA_d[1], 128), (2, ohA_d[2], 128), (3, ohC_d, 64),
                        (4, ohM_d[0], 17), (5, ohM_d[1], 17), (6, ohM_d[2], 17)]
            for v, src, nk in variants:
                fw.dma(sp, ohb[0:34, 0:64 * nk], src, writes=[("s", "ohb")])
                for b in range(4):
                    fns = []
                    for il in range(16):
                        i = b * 16 + il
                        fns.append(lambda i=i, il=il, b=b, nk=nk: nc.tensor.matmul(
                            ps[b][0:nk, il * 32:(il + 1) * 32], ohb[0:34, i * nk:(i + 1) * nk], taug[0:34, :],
                            start=True, stop=True))
                    fw.pe_group(fns, reads=[("s", "ohb"), "taug"], writes=[PSK(b)])
                    o_ap = bt[0:nk, :].rearrange("s (h i) -> s h i", i=64)[:, :, b * 16:(b + 1) * 16]
                    i_ap = ps[b][0:nk, :].rearrange("s (i h) -> s h i", h=32)
                    fw.op(dve, lambda o_ap=o_ap, i_ap=i_ap: nc.vector.tensor_copy(out=o_ap, in_=i_ap),
                          reads=[PSK(b)], writes=[("s", "bt")])
                fw.dma(sp, biasd[v, 0:nk, :], bt[0:nk, :], reads=[("s", "bt")], writes=[("biasd", v)])

        build_bias()

        ring = {"pos": 0}
        cur = {"t": 0}
        NCAST = 4

        stgc = {"n": 0}

        def wload(key, n, tileS, stg=None):
            s = ring["pos"] % NRING
            ring["pos"] += 1
            nsl, el = wst[key]
            pieces = wsrc(key, n)
            wkeys = [("w", s, 0), ("w", s, 1)]
            t = cur["t"]
            if t <= NCAST:
                for a, (lo, hi, (kk, cc), src) in enumerate(pieces):
                    dst = wring[:, s, lo:hi].rearrange("p (k c) -> p k c", c=cc)
                    wk_ = [("w", s, a)] if len(pieces) == 2 else wkeys
                    if stg is not None and n % 2 == 1:
                        i_ = stgc["n"] % len(stg)
                        stgc["n"] += 1
                        sv = stg[i_][:, 0:hi - lo].rearrange("p (k c) -> p k c", c=cc)
                        fw.dma(sp, sv, src, writes=[("s", "wstg", i_)])
                        fw.op(act, lambda dst=dst, sv=sv: nc.scalar.copy(out=dst, in_=sv),
                              reads=[("s", "wstg", i_)], writes=wk_)
                    else:
                        fw.dma(pool, dst, src, writes=wk_)
                if t >= 1 and (n % NCAST) == (t - 1):
                    fw.dma(sp, wscr[key][n], wring[:, s, 0:el], reads=wkeys, writes=[("wscr", key, n)])
            else:
                fw.dma(sp, wring[:, s, 0:el], wscr[key][n], reads=[("wscr", key, n)], writes=wkeys)
            return s, wkeys

        def emit_norm(tile, c0, n, gidx, out_fn, out_keys):
            S.reset()
            sq = [S.bf16(TN) for _ in range(4)]
            rstd = S.f32(TN)
            for kc in range(KD):
                b = kc % 4
                fw.op(act, lambda kc=kc, b=b: nc.scalar.activation(out=sq[b][:, 0:n], in_=xT[:, kc, c0:c0 + n],
                                                                    func=AF.Square),
                      reads=[("xT", kc)], writes=[("s", "sq", b)])
                fw.pe_group([lambda kc=kc, b=b: nc.tensor.matmul(ps[6][:, 0:n], onesD[:], sq[b][:, 0:n],
                                                                  start=(kc == 0), stop=(kc == KD - 1))],
                            reads=[("s", "sq", b), "onesD"], writes=[PSK(6)])
            fw.op(act, lambda: nc.scalar.activation(out=rstd[:, 0:n], in_=ps[6][:, 0:n], func=AF.Ln,
                                                    bias=epsT[:, 0:1], scale=1.0),
                  reads=[PSK(6), "epsT"], writes=[("s", "rstd")])
            fw.op(act, lambda: nc.scalar.activation(out=rstd[:, 0:n], in_=rstd[:, 0:n], func=AF.Exp, scale=-0.5),
                  reads=[("s", "rstd")], writes=[("s", "rstd")])
            for kc in range(KD):
                fw.op(dve, lambda kc=kc: nc.vector.scalar_tensor_tensor(
                    out=out_fn(kc), in0=xT[:, kc, c0:c0 + n], scalar=gains[:, gidx, kc:kc + 1], in1=rstd[:, 0:n],
                    op0=ALU.mult, op1=ALU.mult),
                    reads=[("xT", kc), "gains", ("s", "rstd")], writes=[out_keys(kc)])

        def hT_out(n):
            return (lambda kc: hT[:, kc, 0:n]), (lambda kc: ("hT", kc))

        def emit_ffn(tileS, f, c0, n):
            of, ok = hT_out(n)
            emit_norm(tileS, c0, n, f, of, ok)
            S.reset()
            actT = S.bf16(KF * n).rearrange("p (k t) -> p k t", t=n)
            sg = [S.f32(TN) for _ in range(2)]
            stg = [S.f32(5632) for _ in range(2)] if cur["t"] == 0 else None
            hkeys = [("hT", kc) for kc in range(KD)]
            for g in range(22):
                s, wk = wload(("gu", f), g, tileS, stg)
                wv = wring[:, s, :].rearrange("p (a k c) -> p a k c", a=2, k=16)
                for j in range(2):
                    fc = 2 * g + j
                    bg, bu = (0, 1) if fc % 2 == 0 else (2, 3)
                    for a, bank in ((0, bg), (1, bu)):
                        fns = [lambda kc=kc, a=a, bank=bank, j=j, wv=wv: nc.tensor.matmul(
                            ps[bank][:, 0:n], wv[:, a, kc, j * 128:(j + 1) * 128], hT[:, kc, 0:n],
                            start=(kc == 0), stop=(kc == KD - 1)) for kc in range(KD)]
                        fw.pe_group(fns, reads=wk + hkeys, writes=[PSK(bank)])
                    sgi = sg[fc % 2]
                    fw.op(act, lambda sgi=sgi, bg=bg: nc.scalar.activation(out=sgi[:, 0:n], in_=ps[bg][:, 0:n],
                                                                           func=AF.Silu),
                          reads=[PSK(bg)], writes=[("s", "sg", fc % 2)])
                    fw.op(dve, lambda sgi=sgi, bu=bu, fc=fc: nc.vector.tensor_tensor(
                        out=actT[:, fc, 0:n], in0=ps[bu][:, 0:n], in1=sgi[:, 0:n], op=ALU.mult),
                        reads=[PSK(bu), ("s", "sg", fc % 2)], writes=[("s", "actT", fc)])
            for fp in range(8):
                banks = (4, 5) if fp % 2 == 0 else (6, 7)
                for hh in range(2):
                    s, wk = wload(("dn", f), 2 * fp + hh, tileS, stg)
                    wv = wring[:, s, 0:5632].rearrange("p (k c) -> p k c", c=256)
                    for j in range(2):
                        fns = [lambda kl=kl, hh=hh, j=j, wv=wv, bank=banks[j]: nc.tensor.matmul(
                            ps[bank][:, 0:n], wv[:, kl, j * 128:(j + 1) * 128], actT[:, hh * 22 + kl, 0:n],
                            start=(hh == 0 and kl == 0), stop=(hh == 1 and kl == 21)) for kl in range(22)]
                        fw.pe_group(fns, reads=wk + [("s", "actT", hh * 22 + kl) for kl in range(22)],
                                    writes=[PSK(banks[j])])
                for j in range(2):
                    fo = 2 * fp + j
                    fw.op(dve, lambda fo=fo, bank=banks[j]: nc.vector.scalar_tensor_tensor(
                        out=xT[:, fo, c0:c0 + n], in0=ps[bank][:, 0:n], scalar=0.5, in1=xT[:, fo, c0:c0 + n],
                        op0=ALU.mult, op1=ALU.add),
                        reads=[PSK(banks[j]), ("xT", fo)], writes=[("xT", fo)])

        uid = {"n": 0}

        def transpose_out(src_fn, ntok, dst, q, key):
            ost = S.f32(D)
            uid["n"] += 1
            oid = uid["n"]
            for g4 in range(4):
                bank = g4 % 2
                fns = [lambda kc=kc, bank=bank: nc.tensor.transpose(
                    ps[bank][0:ntok, (kc % 4) * 128:(kc % 4 + 1) * 128], src_fn(kc), ident[:])
                    for kc in range(4 * g4, 4 * g4 + 4)]
                fw.pe_group(fns, reads=[key(kc) for kc in range(4 * g4, 4 * g4 + 4)] + ["ident"],
                            writes=[PSK(bank)])
                fw.op(act, lambda g4=g4, bank=bank: nc.scalar.copy(out=ost[0:ntok, g4 * 512:(g4 + 1) * 512],
                                                                    in_=ps[bank][0:ntok, :]),
                      reads=[PSK(bank)], writes=[("s", "ost", oid, g4)])
            fw.dma(q, dst, ost[0:ntok, :], reads=[("s", "ost", oid, g4) for g4 in range(4)],
                   writes=[("out", oid)])

        def emit_input(tileS, t, mq):
            S.reset()
            if tileS:
                blocks = [(0, 128), (128, 128), (256, 12)]
                src = xs
                r0 = 0
            else:
                blocks = [(0, 128), (128, 128), (256, 128), (384, 128)]
                src = xm
                r0 = (t - 1) * TN
            xin = S.f32(4 * D).rearrange("p (b c) -> p b c", c=D)
            for bi, (o, nt_) in enumerate(blocks):
                fw.dma(mq, xin[0:nt_, bi, :], src[r0 + o:r0 + o + nt_, :], writes=[("s", "xin", bi)])
            for kc in range(KD):
                bank = kc % 2
                fns = [lambda bi=bi, o=o, nt_=nt_, kc=kc, bank=bank: nc.tensor.transpose(
                    ps[bank][:, o:o + nt_], xin[0:nt_, bi, kc * 128:(kc + 1) * 128], ident[0:nt_, 0:nt_])
                    for bi, (o, nt_) in enumerate(blocks)]
                fw.pe_group(fns, reads=[("s", "xin", bi) for bi in range(len(blocks))] + ["ident"],
                            writes=[PSK(bank)])
                ncols = blocks[-1][0] + blocks[-1][1]
                fw.op(act if kc % 2 == 0 else dve,
                      (lambda kc=kc, bank=bank, ncols=ncols: nc.scalar.copy(out=xT[:, kc, 0:ncols], in_=ps[bank][:, 0:ncols]))
                      if kc % 2 == 0 else
                      (lambda kc=kc, bank=bank, ncols=ncols: nc.vector.tensor_copy(out=xT[:, kc, 0:ncols], in_=ps[bank][:, 0:ncols])),
                      reads=[PSK(bank)], writes=[("xT", kc)])

        def emit_conv(tileS, last, n, mq):
            of, ok = hT_out(n)
            emit_norm(tileS, 0, n, 4, of, ok)
            S.reset()
            if tileS:
                UW = 358; W = 328
                segs = [(0, 188, 30), (188, 16, 248), (204, 64, 294)]
                hsrc = 188
            else:
                UW = 542; W = 512
                segs = [(0, 512, 30)]
                hsrc = 512
            U = S.bf16(KD * UW).rearrange("p (k t) -> p k t", t=UW)
            Y = S.f32(KD * W).rearrange("p (k t) -> p k t", t=W)
            dg = [S.bf16(16 * P).rearrange("p (w c) -> p w c", c=P) for _ in range(2)]
            sig = [S.f32(TN) for _ in range(2)]
            ybf = [S.bf16(TN) for _ in range(2)]
            y2 = [S.bf16(TN) for _ in range(2)]
            m2 = S.f32(TN); rstd = S.f32(TN); nmr = S.f32(TN)
            tt = [S.f32(TN) for _ in range(2)]
            UK = lambda fc: ("s", "U", fc)
            if tileS:
                fw.op(dve, lambda: nc.vector.memset(U[:, :, 0:30], 0.0), writes=[UK(fc) for fc in range(KD)])
                fw.op(dve, lambda: nc.vector.memset(U[:, :, 218:248], 0.0),
                      reads=[UK(0)], writes=[("s", "Upad")])
                sct = S.f32(D)
                fw.dma(mq, sct[0:30, :], sconv_d, writes=[("s", "sct")])
                for fc in range(KD):
                    bank = 4 + fc % 2
                    fw.pe_group([lambda fc=fc, bank=bank: nc.tensor.transpose(
                        ps[bank][:, 0:30], sct[0:30, fc * 128:(fc + 1) * 128], ident[0:30, 0:30])],
                        reads=[("s", "sct"), "ident"], writes=[PSK(bank)])
                    fw.op(act, lambda fc=fc, bank=bank: nc.scalar.copy(out=U[:, fc, 264:294], in_=ps[bank][:, 0:30]),
                          reads=[PSK(bank), ("s", "Upad")], writes=[("s", "Uh", fc)])
            else:
                fw.op(act, lambda: nc.scalar.copy(out=U[:, :, 0:30], in_=hist[:]), reads=["hist"],
                      writes=[UK(fc) for fc in range(KD)])
            hkeys = [("hT", kc) for kc in range(KD)]
            for g in range(8):
                s, wk = wload("pw1", g, tileS)
                wv = wring[:, s, :].rearrange("p (a k c) -> p a k c", a=2, k=16)
                for j in range(2):
                    fc = 2 * g + j
                    ba, bg = (0, 1) if fc % 2 == 0 else (2, 3)
                    for a, bank in ((0, ba), (1, bg)):
                        fns = [lambda kc=kc, a=a, bank=bank, j=j, wv=wv: nc.tensor.matmul(
                            ps[bank][:, 0:n], wv[:, a, kc, j * 128:(j + 1) * 128], hT[:, kc, 0:n],
                            start=(kc == 0), stop=(kc == KD - 1)) for kc in range(KD)]
                        fw.pe_group(fns, reads=wk + hkeys, writes=[PSK(bank)])
                    sgi = sig[fc % 2]
                    fw.op(act, lambda sgi=sgi, bg=bg, fc=fc: nc.scalar.activation(
                        out=sgi[:, 0:n], in_=ps[bg][:, 0:n], func=AF.Sigmoid, bias=bpw1[:, 16 + fc:17 + fc], scale=1.0),
                        reads=[PSK(bg), "bpw1"], writes=[("s", "sig", fc % 2)])
                    for (c_, l_, u_) in segs:
                        fw.op(dve, lambda sgi=sgi, ba=ba, fc=fc, c_=c_, l_=l_, u_=u_: nc.vector.scalar_tensor_tensor(
                            out=U[:, fc, u_:u_ + l_], in0=ps[ba][:, c_:c_ + l_], scalar=bpw1[:, fc:fc + 1],
                            in1=sgi[:, c_:c_ + l_], op0=ALU.add, op1=ALU.mult),
                            reads=[PSK(ba), ("s", "sig", fc % 2), "bpw1", UK(fc), ("s", "Uh", fc), ("s", "Upad")],
                            writes=[UK(fc)])
                    if tileS:
                        fw.op(dve, lambda fc=fc: nc.vector.tensor_tensor(
                            out=U[:, fc, 30:218], in0=U[:, fc, 30:218], in1=umask[:, 0:188], op=ALU.mult),
                            reads=[UK(fc), "umask"], writes=[UK(fc)])
            fw.op(act, lambda: nc.scalar.copy(out=hist[:], in_=U[:, :, hsrc:hsrc + 30]),
                  reads=[UK(fc) for fc in range(KD)], writes=["hist"])
            if tileS:
                cst = S.f32(KD * 30).rearrange("p (k t) -> p k t", t=30)
                fw.op(act, lambda: nc.scalar.copy(out=cst[:], in_=U[:, :, 328:358]),
                      reads=[UK(fc) for fc in range(KD)], writes=[("s", "cst", kc_) for kc_ in range(KD)])
                transpose_out(lambda kc: cst[:, kc, :], 30, cs_s, mq, lambda kc: ("s", "cst", kc))
            YK = lambda fc: ("s", "Y", fc)
            for fc in range(KD):
                for hf, (w0, nw) in enumerate(((0, 16), (16, 15))):
                    fw.op(dve, lambda fc=fc, hf=hf, w0=w0, nw=nw: nc.vector.tensor_tensor(
                        out=dg[hf][:, 0:nw, :], in0=identb[:].unsqueeze(1).broadcast_to([P, nw, P]),
                        in1=wdw[:, fc, w0:w0 + nw].unsqueeze(2).broadcast_to([P, nw, P]), op=ALU.mult),
                        reads=["identb", "wdw"], writes=[("s", "dg", hf)])
                bank = 4 + fc % 2
                for hf, (w0, nw) in enumerate(((0, 16), (16, 15))):
                    fns = [lambda fc=fc, hf=hf, wl=wl, w0=w0, bank=bank: nc.tensor.matmul(
                        ps[bank][:, 0:W], dg[hf][:, wl, :], U[:, fc, w0 + wl:w0 + wl + W],
                        start=(w0 + wl == 0), stop=(w0 + wl == 30)) for wl in range(nw)]
                    fw.pe_group(fns, reads=[("s", "dg", hf), UK(fc), ("s", "Uh", fc), ("s", "Upad")], writes=[PSK(bank)])
                b2 = fc % 2
                fw.op(dve, lambda fc=fc, bank=bank: nc.vector.tensor_scalar(
                    out=Y[:, fc, :], in0=ps[bank][:, 0:W], scalar1=gains[:, 9, fc:fc + 1], scalar2=None, op0=ALU.add),
                    reads=[PSK(bank), "gains"], writes=[YK(fc)])
                fw.op(act, lambda fc=fc, b2=b2: nc.scalar.copy(out=ybf[b2][:, 0:W], in_=Y[:, fc, :]),
                      reads=[YK(fc)], writes=[("s", "ybf", b2)])
                fw.op(act, lambda fc=fc, b2=b2: nc.scalar.activation(out=y2[b2][:, 0:W], in_=Y[:, fc, :], func=AF.Square),
                      reads=[YK(fc)], writes=[("s", "y2", b2)])
                fw.pe_group([lambda fc=fc, b2=b2: nc.tensor.matmul(ps[6][:, 0:W], onesD[:], ybf[b2][:, 0:W],
                                                                   start=(fc == 0), stop=(fc == KD - 1))],
                            reads=[("s", "ybf", b2), "onesD"], writes=[PSK(6)])
                fw.pe_group([lambda fc=fc, b2=b2: nc.tensor.matmul(ps[7][:, 0:W], onesD[:], y2[b2][:, 0:W],
                                                                   start=(fc == 0), stop=(fc == KD - 1))],
                            reads=[("s", "y2", b2), "onesD"], writes=[PSK(7)])
            fw.op(dve, lambda: nc.vector.tensor_copy(out=m2[:, 0:W], in_=ps[6][:, 0:W]),
                  reads=[PSK(6)], writes=[("s", "m2")])
            fw.op(dve, lambda: nc.vector.tensor_tensor(out=nmr[:, 0:W], in0=m2[:, 0:W], in1=m2[:, 0:W], op=ALU.mult),
                  reads=[("s", "m2")], writes=[("s", "nmr")])
            fw.op(dve, lambda: nc.vector.tensor_tensor(out=rstd[:, 0:W], in0=ps[7][:, 0:W], in1=nmr[:, 0:W],
                                                       op=ALU.subtract),
                  reads=[PSK(7), ("s", "nmr")], writes=[("s", "rstd")])
            fw.op(act, lambda: nc.scalar.activation(out=rstd[:, 0:W], in_=rstd[:, 0:W], func=AF.Ln,
                                                    bias=epsT[:, 0:1], scale=1.0),
                  reads=[("s", "rstd"), "epsT"], writes=[("s", "rstd")])
            fw.op(act, lambda: nc.scalar.activation(out=rstd[:, 0:W], in_=rstd[:, 0:W], func=AF.Exp, scale=-0.5),
                  reads=[("s", "rstd")], writes=[("s", "rstd")])
            fw.op(dve, lambda: nc.vector.scalar_tensor_tensor(out=nmr[:, 0:W], in0=m2[:, 0:W], scalar=-1.0,
                                                              in1=rstd[:, 0:W], op0=ALU.mult, op1=ALU.mult),
                  reads=[("s", "m2"), ("s", "rstd"), ("s", "nmr")], writes=[("s", "nmr")])
            for fc in range(KD):
                t_ = tt[fc % 2]
                fw.op(dve, lambda fc=fc, t_=t_: nc.vector.tensor_tensor(out=t_[:, 0:W], in0=Y[:, fc, :],
                                                                        in1=rstd[:, 0:W], op=ALU.mult),
                      reads=[YK(fc), ("s", "rstd")], writes=[("s", "tt", fc % 2)])
                fw.op(dve, lambda fc=fc, t_=t_: nc.vector.tensor_tensor(out=t_[:, 0:W], in0=t_[:, 0:W],
                                                                        in1=nmr[:, 0:W], op=ALU.add),
                      reads=[("s", "tt", fc % 2), ("s", "nmr")], writes=[("s", "tt", fc % 2)])
                fw.op(act, lambda fc=fc, t_=t_: nc.scalar.activation(
                    out=hT[:, fc, 0:W], in_=t_[:, 0:W], func=AF.Silu, bias=gains[:, 8, fc:fc + 1],
                    scale=gains[:, 7, fc:fc + 1]),
                    reads=[("s", "tt", fc % 2), "gains"], writes=[("hT", fc)])
            if tileS:
                osegs = [(0, 188, 0), (188, 16, 218), (204, 64, 264)]
            else:
                osegs = [(0, 512, 0)]
            for g in range(8):
                s, wk = wload("pw2", g, tileS)
                wv = wring[:, s, 0:4096].rearrange("p (k c) -> p k c", c=256)
                for j in range(2):
                    fo = 2 * g + j
                    bank = 4 + fo % 2
                    fns = [lambda kc=kc, bank=bank, j=j, wv=wv: nc.tensor.matmul(
                        ps[bank][:, 0:W], wv[:, kc, j * 128:(j + 1) * 128], hT[:, kc, 0:W],
                        start=(kc == 0), stop=(kc == KD - 1)) for kc in range(KD)]
                    fw.pe_group(fns, reads=wk + hkeys, writes=[PSK(bank)])
                    for (c_, l_, o_) in osegs:
                        fw.op(dve, lambda fo=fo, bank=bank, c_=c_, l_=l_, o_=o_: nc.vector.scalar_tensor_tensor(
                            out=xT[:, fo, c_:c_ + l_], in0=ps[bank][:, o_:o_ + l_], scalar=gains[:, 10, fo:fo + 1],
                            in1=xT[:, fo, c_:c_ + l_], op0=ALU.add, op1=ALU.add),
                            reads=[PSK(bank), "gains", ("xT", fo)], writes=[("xT", fo)])

        def small_out(psk, src_ps, nrow, ncol, dst, q, dedup=False):
            stg = S.f32(256)
            uid["n"] += 1
            key = ("s", "stg", uid["n"])
            if dedup:
                o_ = stg[0:nrow, 0:256].rearrange("p (k d) -> p k d", d=64)
                i_ = src_ps.rearrange("p (k a d) -> p k a d", a=2, d=64)[:, :, 0, :]
            else:
                o_ = stg[0:nrow, 0:ncol]; i_ = src_ps
            fw.op(act, lambda: nc.scalar.copy(out=o_, in_=i_),
                  reads=[psk], writes=[key])
            fw.dma(q, dst, stg[0:nrow, 0:ncol], reads=[key], writes=[("out", uid["n"])])

        def emit_kv(tileS, last, n, mq):
            S.reset()
            hkv = S.bf16(KD * 640).rearrange("p (k t) -> p k t", t=640)
            HK = lambda kc: ("s", "hkv", kc)
            fw.op(act, lambda: nc.scalar.copy(out=hkv[:, :, 0:128], in_=hprev[:]), reads=["hprev"],
                  writes=[HK(kc) for kc in range(KD)])
            sq = [S.bf16(TN) for _ in range(4)]
            rstd = S.f32(TN)
            for kc in range(KD):
                b = kc % 4
                fw.op(act, lambda kc=kc, b=b: nc.scalar.activation(out=sq[b][:, 0:n], in_=xT[:, kc, 0:n], func=AF.Square),
                      reads=[("xT", kc)], writes=[("s", "sq", b)])
                fw.pe_group([lambda kc=kc, b=b: nc.tensor.matmul(ps[6][:, 0:n], onesD[:], sq[b][:, 0:n],
                                                                  start=(kc == 0), stop=(kc == KD - 1))],
                            reads=[("s", "sq", b), "onesD"], writes=[PSK(6)])
            fw.op(act, lambda: nc.scalar.activation(out=rstd[:, 0:n], in_=ps[6][:, 0:n], func=AF.Ln,
                                                    bias=epsT[:, 0:1], scale=1.0),
                  reads=[PSK(6), "epsT"], writes=[("s", "rstd")])
            fw.op(act, lambda: nc.scalar.activation(out=rstd[:, 0:n], in_=rstd[:, 0:n], func=AF.Exp, scale=-0.5),
                  reads=[("s", "rstd")], writes=[("s", "rstd")])
            for kc in range(KD):
                fw.op(dve, lambda kc=kc: nc.vector.scalar_tensor_tensor(
                    out=hkv[:, kc, 128:128 + n], in0=xT[:, kc, 0:n], scalar=gains[:, 5, kc:kc + 1], in1=rstd[:, 0:n],
                    op0=ALU.mult, op1=ALU.mult),
                    reads=[("xT", kc), "gains", ("s", "rstd"), HK(kc)], writes=[HK(kc)])
            hk_all = [HK(kc) for kc in range(KD)]
            s, wk = wload("wk", 0, tileS)
            wkv = wring[:, s, :].rearrange("p (k c) -> p k c", c=512)
            ksq = [S.bf16(TN) for _ in range(2)]
            krs = [S.f32(TN) for _ in range(2)]
            knf = [S.f32(TN) for _ in range(2)]
            for k in range(4):
                b2 = k % 2
                bk = 4 + b2
                fns = [lambda kc=kc, k=k, bk=bk: nc.tensor.matmul(
                    ps[bk][:, 0:n], wkv[:, kc, k * 128:(k + 1) * 128], hkv[:, kc, 128:128 + n],
                    start=(kc == 0), stop=(kc == KD - 1)) for kc in range(KD)]
                fw.pe_group(fns, reads=wk + hk_all, writes=[PSK(bk)])
                fw.op(act, lambda b2=b2, bk=bk: nc.scalar.activation(out=ksq[b2][:, 0:n], in_=ps[bk][:, 0:n], func=AF.Square),
                      reads=[PSK(bk)], writes=[("s", "ksq", b2)])
                fw.pe_group([lambda b2=b2: nc.tensor.matmul(ps[6 + b2][:, 0:n], bd64[:], ksq[b2][:, 0:n], start=True, stop=True)],
                            reads=[("s", "ksq", b2), "bd64"], writes=[PSK(6 + b2)])
                fw.op(act, lambda b2=b2: nc.scalar.activation(out=krs[b2][:, 0:n], in_=ps[6 + b2][:, 0:n], func=AF.Ln,
                                                              bias=epsT[:, 0:1], scale=1.0),
                      reads=[PSK(6 + b2), "epsT"], writes=[("s", "krs", b2)])
                fw.op(act, lambda b2=b2: nc.scalar.activation(out=krs[b2][:, 0:n], in_=krs[b2][:, 0:n], func=AF.Exp, scale=-0.5),
                      reads=[("s", "krs", b2)], writes=[("s", "krs", b2)])
                fw.op(dve, lambda b2=b2, bk=bk: nc.vector.scalar_tensor_tensor(
                    out=knf[b2][:, 0:n], in0=ps[bk][:, 0:n], scalar=kqn[:, 0:1], in1=krs[b2][:, 0:n],
                    op0=ALU.mult, op1=ALU.mult),
                    reads=[PSK(bk), "kqn", ("s", "krs", b2)], writes=[("s", "knf", b2)])
                KNF = ("s", "knf", b2)
                if tileS:
                    fw.op(act, lambda k=k, b2=b2: nc.scalar.copy(out=KT[:, k, 0:16], in_=knf[b2][:, 188:204]),
                          reads=[KNF, "KT"], writes=[("KT", k)])
                    fw.op(act, lambda k=k, b2=b2: nc.scalar.copy(out=KT[:, k, 17:145], in_=knf[b2][:, 60:188]),
                          reads=[KNF, ("KT", k)], writes=[("KT", k)])
                    fw.op(act, lambda k=k, b2=b2: nc.scalar.copy(out=KTs[:, k, 145:209], in_=knf[b2][:, 204:268]),
                          reads=[KNF, "KTs"], writes=[("KTs", k)])
                    for (c_, nt_, dst) in ((204, 64, knew_o), (188, 16, kmeta_o)):
                        bt_ = 2 + (k % 2)
                        fw.pe_group([lambda b2=b2, c_=c_, nt_=nt_, bt_=bt_: nc.tensor.transpose(
                            ps[bt_][0:nt_, 0:64], knf[b2][0:64, c_:c_ + nt_], ident[0:64, 0:64])],
                            reads=[KNF, "ident"], writes=[PSK(bt_)])
                        small_out(PSK(bt_), ps[bt_][0:nt_, 0:64], nt_, 64, dst[:, k * 64:(k + 1) * 64], mq)
                else:
                    fw.op(act, lambda k=k, b2=b2: nc.scalar.copy(out=KT[:, k, 145:145 + TN], in_=knf[b2][:, 0:TN]),
                          reads=[KNF, "KT", ("KT", k)], writes=[("KT", k)])
                    if last:
                        bt_ = 2 + (k % 2)
                        fw.pe_group([lambda b2=b2, bt_=bt_: nc.tensor.transpose(
                            ps[bt_][0:128, 0:64], knf[b2][0:64, 384:512], ident[0:64, 0:64])],
                            reads=[KNF, "ident"], writes=[PSK(bt_)])
                        small_out(PSK(bt_), ps[bt_][0:128, 0:64], 128, 64, kwin_o[:, k * 64:(k + 1) * 64], mq)
            s, wk = wload("wv", 0, tileS)
            wvv = wring[:, s, :].rearrange("p (k c) -> p k c", c=512)

            def vmm(bank, c_, m_):
                fns = [lambda kc=kc: nc.tensor.matmul(ps[bank][0:m_, 0:512], hkv[:, kc, c_:c_ + m_], wvv[:, kc, :],
                                                      start=(kc == 0), stop=(kc == KD - 1)) for kc in range(KD)]
                fw.pe_group(fns, reads=wk + hk_all, writes=[PSK(bank)])
            if tileS:
                vmm(0, 128 + META0, 16)
                fw.op(act, lambda: nc.scalar.copy(out=Vmeta[0:16, :], in_=ps[0][0:16, 0:512]),
                      reads=[PSK(0), "Vmeta"], writes=["Vmeta"])
                small_out(PSK(0), ps[0][0:16, 0:512], 16, 256, vmeta_o, mq, dedup=True)
                vmm(1, 128 + SMP0, 64)
                fw.op(act, lambda: nc.scalar.copy(out=VsC[0:64, :], in_=ps[1][0:64, 0:512]),
                      reads=[PSK(1)], writes=["VsC"])
                small_out(PSK(1), ps[1][0:64, 0:512], 64, 256, vnew_o, mq, dedup=True)
                cvt = S.f32(512)
                fw.dma(mq, cvt[:, 0:256], cv_d[16:144, :], writes=[("s", "cvw")])
                fw.dma(mq, cvt[0:16, 256:512], cv_d[0:16, :], writes=[("s", "cvm")])
                for a_ in range(2):
                    fw.op(dve, lambda a_=a_: nc.vector.tensor_copy(
                        out=VsA[:].rearrange("p (k a d) -> p k a d", a=2, d=64)[:, :, a_, :],
                        in_=cvt[:, 0:256].rearrange("p (k d) -> p k d", d=64)), reads=[("s", "cvw"), "VsA"], writes=["VsA"])
                    fw.op(dve, lambda a_=a_: nc.vector.tensor_copy(
                        out=VsM[0:16, :].rearrange("p (k a d) -> p k a d", a=2, d=64)[:, :, a_, :],
                        in_=cvt[0:16, 256:512].rearrange("p (k d) -> p k d", d=64)),
                        reads=[("s", "cvm"), "VsM"], writes=["VsM"])
                ckw = S.f32(512); ckm = S.f32(512)
                fw.dma(mq, ckw[:, :], ckd_d[16:144, :], writes=[("s", "ckw")])
                fw.dma(mq, ckm[0:16, :], ckd_d[0:16, :], writes=[("s", "ckm")])
                for k in range(4):
                    bt_ = 2 + (k % 2)
                    fw.pe_group([lambda k=k, bt_=bt_: nc.tensor.transpose(ps[bt_][:, 0:128], ckw[:, k * 128:(k + 1) * 128], ident[:]),
                                 lambda k=k, bt_=bt_: nc.tensor.transpose(ps[bt_][:, 128:144], ckm[0:16, k * 128:(k + 1) * 128],
                                                                          ident[0:16, 0:16])],
                                reads=[("s", "ckw"), ("s", "ckm"), "ident"], writes=[PSK(bt_)])
                    fw.op(act, lambda k=k, bt_=bt_: nc.scalar.copy(out=KTs[:, k, 17:145], in_=ps[bt_][:, 0:128]),
                          reads=[PSK(bt_), "KTs", ("KTs", k)], writes=[("KTs", k)])
                    fw.op(act, lambda k=k, bt_=bt_: nc.scalar.copy(out=KTs[:, k, 0:16], in_=ps[bt_][:, 128:144]),
                          reads=[PSK(bt_), ("KTs", k)], writes=[("KTs", k)])
                fw.op(act, lambda: nc.scalar.copy(out=hprev[:], in_=hkv[:, :, 128 + 60:128 + 188]),
                      reads=hk_all, writes=["hprev"])
            else:
                for m in range(10):
                    bank = m % 2
                    m_ = 128 if m < 9 else 64
                    vmm(bank, 64 * m, m_)
                    fw.op(act if m % 2 == 0 else dve,
                          (lambda m=m, bank=bank, m_=m_: nc.scalar.copy(out=Vatt[0:m_, m, :], in_=ps[bank][0:m_, 0:512]))
                          if m % 2 == 0 else
                          (lambda m=m, bank=bank, m_=m_: nc.vector.tensor_copy(out=Vatt[0:m_, m, :], in_=ps[bank][0:m_, 0:512])),
                          reads=[PSK(bank)], writes=[("Vatt", m)])
                    if last and m == 8:
                        small_out(PSK(bank), ps[bank][0:128, 0:512], 128, 256, vwin_o, mq, dedup=True)
                fw.op(act, lambda: nc.scalar.copy(out=hprev[:], in_=hkv[:, :, 512:640]), reads=hk_all, writes=["hprev"])

        def emit_attn(tileS, first, c0, n, mq):
            of, ok = hT_out(n)
            emit_norm(tileS, c0, n, 6, of, ok)
            S.reset()
            qTe = S.bf16(KD * TN).rearrange("p (k t) -> p k t", t=TN)
            qTo = S.bf16(KD * TN).rearrange("p (k t) -> p k t", t=TN)
            qTh = (qTe, qTo)
            QK = lambda qc: ("s", "qT", qc)
            fw.op(dve, lambda: nc.vector.memset(qTe[64:128, :, :], 0.0), writes=[("s", "qz", 0)])
            fw.op(dve, lambda: nc.vector.memset(qTo[0:64, :, :], 0.0), writes=[("s", "qz", 1)])
            qsq = [S.bf16(TN) for _ in range(2)]
            qrs = [S.f32(TN) for _ in range(2)]
            hkeys = [("hT", kc) for kc in range(KD)]
            for g in range(8):
                s, wk = wload("wq", g, tileS)
                wv = wring[:, s, 0:4096].rearrange("p (k c) -> p k c", c=256)
                for j in range(2):
                    qc = 2 * g + j
                    b2 = qc % 2
                    bank = 4 + b2
                    fns = [lambda kc=kc, bank=bank, j=j, wv=wv: nc.tensor.matmul(
                        ps[bank][:, 0:n], wv[:, kc, j * 128:(j + 1) * 128], hT[:, kc, 0:n],
                        start=(kc == 0), stop=(kc == KD - 1)) for kc in range(KD)]
                    fw.pe_group(fns, reads=wk + hkeys, writes=[PSK(bank)])
                    fw.op(act, lambda b2=b2, bank=bank: nc.scalar.activation(out=qsq[b2][:, 0:n], in_=ps[bank][:, 0:n], func=AF.Square),
                          reads=[PSK(bank)], writes=[("s", "qsq", b2)])
                    fw.pe_group([lambda b2=b2: nc.tensor.matmul(ps[6 + b2][:, 0:n], bd64[:], qsq[b2][:, 0:n], start=True, stop=True)],
                                reads=[("s", "qsq", b2), "bd64"], writes=[PSK(6 + b2)])
                    fw.op(act, lambda b2=b2: nc.scalar.activation(out=qrs[b2][:, 0:n], in_=ps[6 + b2][:, 0:n], func=AF.Ln,
                                                                  bias=epsT[:, 0:1], scale=1.0),
                          reads=[PSK(6 + b2), "epsT"], writes=[("s", "qrs", b2)])
                    fw.op(act, lambda b2=b2: nc.scalar.activation(out=qrs[b2][:, 0:n], in_=qrs[b2][:, 0:n], func=AF.Exp, scale=-0.5),
                          reads=[("s", "qrs", b2)], writes=[("s", "qrs", b2)])
                    for half in range(2):
                        hs = slice(half * 64, (half + 1) * 64)
                        fw.op(dve, lambda b2=b2, bank=bank, qc=qc, half=half, hs=hs: nc.vector.scalar_tensor_tensor(
                            out=qTh[half][hs, qc, 0:n], in0=ps[bank][hs, 0:n], scalar=qg[hs, 0:1], in1=qrs[b2][hs, 0:n],
                            op0=ALU.mult, op1=ALU.mult),
                            reads=[PSK(bank), "qg", ("s", "qrs", b2), ("s", "qz", 0), ("s", "qz", 1)],
                            writes=[("s", "qT", qc, half)])
            bias_gen = [[S.f32(512) for _ in range(3)] for _ in range(2)]
            bias_sp = [S.f32(512) for _ in range(4)] if first else None
            Tt = [[S.f32(512) for _ in range(3)] for _ in range(2)]
            PT = [[S.bf16(512) for _ in range(3)] for _ in range(2)]
            rd = S.f32(512)
            nch = n // 64
            if tileS:
                Kt = KTs; colA = 17; colC = 145
                ktk = lambda k: ("KTs", k)
            else:
                Kt = KT
                ktk = lambda k: ("KT", k)
            pending = [None]

            def attn_iter(k, lc, bs, st_):
                if tileS:
                    cA, cC = 17, 145
                    vkeys = ["VsA", "VsC", "VsM"]
                    VA = lambda: VsA[0:128, k * 128:(k + 1) * 128]
                    VC = lambda: VsC[0:64, k * 128:(k + 1) * 128]
                    VM = lambda: VsM[0:17, k * 128:(k + 1) * 128]
                else:
                    cA, cC = 17 + 64 * lc, 145 + 64 * lc
                    vkeys = [("Vatt", lc), ("Vatt", lc + 2), "Vmeta"]
                    VA = lambda: Vatt[0:128, lc, k * 128:(k + 1) * 128]
                    VC = lambda: Vatt[0:64, lc + 2, k * 128:(k + 1) * 128]
                    VM = lambda: Vmeta[0:17, k * 128:(k + 1) * 128]
                blocks = [(128, cA, VA), (64, cC, VC), (17, 0, VM)]
                if first and lc < 2:
                    btile = [bias_sp[lc], bias_gen[bs][1], bias_sp[2 + lc]]
                    bkeys = [("s", "bsp", lc), ("s", "bg", bs, 1), ("s", "bsp", 2 + lc)]
                else:
                    btile = bias_gen[bs]
                    bkeys = [("s", "bg", bs, bi) for bi in range(3)]
                qkeys = [("s", "qT", 4 * k + jj, hf_) for jj in range(4) for hf_ in range(2)]
                for bi, (nk, cc, _) in enumerate(blocks):
                    bk = 3 * st_ + bi
                    fns = [lambda half=half, bk=bk, nk=nk, cc=cc: nc.tensor.matmul(
                        ps[bk][0:nk, half * 256:(half + 1) * 256].rearrange("p (a b) -> p a b", b=64),
                        Kt[:, k, cc:cc + nk], qTh[half][:, 4 * k:4 * k + 4, lc * 64:(lc + 1) * 64],
                        start=True, stop=True) for half in range(2)]
                    fw.pe_group(fns, reads=[ktk(k)] + qkeys, writes=[PSK(bk)])
                    fw.op(dve, lambda bi=bi, nk=nk, bk=bk: nc.vector.tensor_tensor(
                        out=Tt[st_][bi][0:nk, :], in0=ps[bk][0:nk, :], in1=btile[bi][0:nk, :], op=ALU.add),
                        reads=[PSK(bk), bkeys[bi]], writes=[("s", "Tt", st_, bi)])
                    fw.op(act, lambda bi=bi, nk=nk: nc.scalar.activation(
                        out=PT[st_][bi][0:nk, :], in_=Tt[st_][bi][0:nk, :], func=AF.Exp),
                        reads=[("s", "Tt", st_, bi)], writes=[("s", "PT", st_, bi)])

                def phase2():
                    ptk = [("s", "PT", st_, bi) for bi in range(3)]
                    PTs = PT[st_]
                    bden, bo = 6, 7
                    fns = [lambda bi=bi, nk=nk: nc.tensor.matmul(
                        ps[bden][:, :], onesB[0:nk, :], PTs[bi][0:nk, :], start=(bi == 0), stop=(bi == 2))
                        for bi, (nk, _, _) in enumerate(blocks)]
                    fw.pe_group(fns, reads=ptk + ["onesB"], writes=[PSK(bden)])
                    fns = [lambda bi=bi, nk=nk, vf=vf: nc.tensor.matmul(
                        ps[bo][:, :], vf(), PTs[bi][0:nk, :], start=(bi == 0), stop=(bi == 2))
                        for bi, (nk, _, vf) in enumerate(blocks)]
                    fw.pe_group(fns, reads=ptk + vkeys, writes=[PSK(bo)])
                    fw.op(act, lambda: nc.scalar.activation(out=rd[:, :], in_=ps[bden][:, :], func=AF.Ln),
                          reads=[PSK(bden)], writes=[("s", "rd")])
                    fw.op(act, lambda: nc.scalar.activation(out=rd[:, :], in_=rd[:, :], func=AF.Exp, scale=-1.0),
                          reads=[("s", "rd")], writes=[("s", "rd")])
                    for half in range(2):
                        fw.op(dve, lambda half=half: nc.vector.tensor_tensor(
                            out=hT[half * 64:(half + 1) * 64, 4 * k:4 * k + 4, lc * 64:(lc + 1) * 64],
                            in0=ps[bo][half * 64:(half + 1) * 64, half * 256:(half + 1) * 256].rearrange("p (a b) -> p a b", b=64),
                            in1=rd[half * 64:(half + 1) * 64, half * 256:(half + 1) * 256].rearrange("p (a b) -> p a b", b=64),
                            op=ALU.mult),
                            reads=[PSK(bo), ("s", "rd")] + [("hT", 4 * k + jj) for jj in range(4)],
                            writes=[("hT", 4 * k + jj) for jj in range(4)])
                return phase2

            cnt = 0
            for k in range(4):
                bs = k % 2
                for bi, v in enumerate((0, 3, 4)):
                    nk = (128, 64, 17)[bi]
                    fw.dma(mq, bias_gen[bs][bi][0:nk, :], biasd[v, 0:nk, k * 512:(k + 1) * 512],
                           reads=[("biasd", v)], writes=[("s", "bg", bs, bi)])
                if first:
                    for bi, v in enumerate((1, 2, 5, 6)):
                        nk = 128 if bi < 2 else 17
                        fw.dma(mq, bias_sp[bi][0:nk, :], biasd[v, 0:nk, k * 512:(k + 1) * 512],
                               reads=[("biasd", v)], writes=[("s", "bsp", bi)])
                for lc in range(nch):
                    ph2 = attn_iter(k, lc, bs, cnt % 2)
                    cnt += 1
                    if pending[0] is not None:
                        pending[0]()
                    pending[0] = ph2
            if pending[0] is not None:
                pending[0]()
            if not tileS:
                fw.op(act, lambda: nc.scalar.copy(out=KT[:, :, 17:145], in_=KT[:, :, 145 + 384:145 + 512]),
                      reads=[("KT", k) for k in range(4)], writes=[("KT", k) for k in range(4)])
            for g in range(8):
                s, wk = wload("wo", g, tileS)
                wv = wring[:, s, 0:4096].rearrange("p (k c) -> p k c", c=256)
                for j in range(2):
                    fo = 2 * g + j
                    bank = 4 + fo % 2
                    fns = [lambda kc=kc, bank=bank, j=j, wv=wv: nc.tensor.matmul(
                        ps[bank][:, 0:n], wv[:, kc, j * 128:(j + 1) * 128], hT[:, kc, 0:n],
                        start=(kc == 0), stop=(kc == KD - 1)) for kc in range(KD)]
                    fw.pe_group(fns, reads=wk + hkeys, writes=[PSK(bank)])
                    fw.op(dve, lambda fo=fo, bank=bank: nc.vector.tensor_tensor(
                        out=xT[:, fo, c0:c0 + n], in0=ps[bank][:, 0:n], in1=xT[:, fo, c0:c0 + n], op=ALU.add),
                        reads=[PSK(bank), ("xT", fo)], writes=[("xT", fo)])

        def emit_output(tileS, t, mq):
            S.reset()
            if tileS:
                transpose_out(lambda kc: xT[:, kc, SMP0:SMP0 + 64], 64, y_smp, mq, lambda kc: ("xT", kc))
            else:
                for b in range(4):
                    r0 = (t - 1) * TN + b * 128
                    transpose_out(lambda kc, b=b: xT[:, kc, b * 128:(b + 1) * 128], 128, y_main[r0:r0 + 128, :], mq,
                                  lambda kc: ("xT", kc))

        import os
        KSTOP = int(os.environ.get("KSTOP", "999"))
        stage_no = [0]

        def go():
            stage_no[0] += 1
            return stage_no[0] <= KSTOP
        for t in range(NT + 1):
            tileS = (t == 0)
            last = (t == NT)
            cur["t"] = t
            mq = sp if t <= NCAST else pool
            n0 = NS if tileS else TN
            c1, n1 = (SMP0, 64) if tileS else (0, TN)
            if go(): emit_input(tileS, t, mq)
            if go(): emit_ffn(tileS, 0, 0, n0)
            if go(): emit_conv(tileS, last, n0, mq)
            if go(): emit_ffn(tileS, 1, 0, n0)
            if go(): emit_kv(tileS, last, n0, mq)
            if go(): emit_ffn(tileS, 2, c1, n1)
            if go(): emit_attn(tileS, t == 1, c1, n1, mq)
            if go(): emit_ffn(tileS, 3, c1, n1)
            if go(): emit_output(tileS, t, mq)
        S.reset()
        transpose_out(lambda kc: hist[:, kc, :], 30, cs_p, pool, lambda kc: "hist")
        for e in (sp, pool, act, dve, pe):
            fw.wait_all(e)
    return nc


def _buckets(rel):
    import jax
    import jax.numpy as jnp
    cpu = jax.devices("cpu")[0]
    with jax.default_device(cpu):
        rel = jnp.asarray(np.asarray(rel, np.int32))
        nb = 16
        max_exact = 8
        n = jnp.abs(rel)
        large = max_exact + (jnp.log(jnp.maximum(n, 1).astype(jnp.float32) / max_exact)
                             / math.log(128 / max_exact) * (nb - max_exact)).astype(jnp.int32)
        large = jnp.minimum(large, nb - 1)
        bucket = jnp.where(rel > 0, nb, 0) + jnp.where(n < max_exact, n, large)
        return np.asarray(bucket)


def _onehots(j):
    i = np.arange(64)
    relA = np.arange(128)[None, :] - 128 - i[:, None]
    bA = _buckets(relA)
    relC = np.arange(64)[None, :] - i[:, None]
    bC = _buckets(relC)

    def oh(b, nk):
        o = np.zeros((34, 64, nk), np.float32)
        ii, ss = np.meshgrid(np.arange(64), np.arange(nk), indexing="ij")
        o[b, ii, ss] = 1.0
        return o
    A_gen = bA.copy()
    A_c0 = bA.copy(); A_c1 = bA.copy()
    if j == 0:
        A_c0[:, :] = 32
        A_c1[:, 0:64] = 32
    ohA = np.stack([oh(A_gen, 128), oh(A_c0, 128), oh(A_c1, 128)]).reshape(3, 34, 64 * 128)
    ohC = oh(bC, 64).reshape(34, 64 * 64)

    def mb(cg):
        qpos = 16 + cg * 64 + i
        rel = np.arange(16)[None, :] - qpos[:, None]
        b = _buckets(rel)
        return np.concatenate([b, np.full((64, 1), 33, b.dtype)], axis=1)
    M_gen = mb(2)
    M_c0 = mb(0) if j == 0 else M_gen
    M_c1 = mb(1) if j == 0 else M_gen
    ohM = np.stack([oh(M_gen, 17), oh(M_c0, 17), oh(M_c1, 17)]).reshape(3, 34, 64 * 17)
    return ohA, ohC, ohM


def _pl(v):
    return np.ascontiguousarray(np.asarray(v, np.float32).reshape(16, 128).T)


_CACHE = {}


def prep_inputs(inputs, NT):
    f32 = lambda a: np.ascontiguousarray(np.asarray(a, dtype=np.float32))
    x_prompt = f32(inputs["x_prompt"]); x_sample = f32(inputs["x_sample"])
    LM = NT * TN
    assert x_prompt.shape[1] == 4 * LM
    meta = f32(inputs["meta_tokens"])
    w_gate = f32(inputs["ffn_w_gate"]).reshape(4, D, DFF)
    w_up = f32(inputs["ffn_w_up"]).reshape(4, D, DFF)
    w_down = f32(inputs["ffn_w_down"]).reshape(4, DFF, D)
    w_pw1 = f32(inputs["conv_w_pw1"])[0]; w_pw2 = f32(inputs["conv_w_pw2"])[0]
    w_k = f32(inputs["w_k"]); w_v = f32(inputs["w_v"])
    w_q = f32(inputs["w_q"])[0]; w_o = f32(inputs["w_o"])[0]
    wkd = np.ascontiguousarray(np.stack([w_k.reshape(D, 4, 64)] * 2, axis=2).reshape(D, 512))
    wvd = np.ascontiguousarray(np.stack([w_v.reshape(D, 4, 64)] * 2, axis=2).reshape(D, 512))
    fn = f32(inputs["ffn_norm"]).reshape(4, D)
    vecs = [fn[0], fn[1], fn[2], fn[3], inputs["conv_norm"][0], inputs["kv_norm"], inputs["attn_norm"][0],
            inputs["conv_ln_g"][0], inputs["conv_ln_b"][0], inputs["conv_b_dw"][0], inputs["conv_b_pw2"][0]]
    gains = np.ascontiguousarray(np.stack([_pl(v) for v in vecs], axis=1).reshape(P, 11 * 16))
    bpw1 = np.ascontiguousarray(f32(inputs["conv_b_pw1"])[0].reshape(32, 128).T)
    wdw = np.ascontiguousarray(f32(inputs["conv_w_dw"])[0].reshape(31, 16, 128).transpose(2, 1, 0).reshape(P, 16 * 31))
    kqn = np.ascontiguousarray(np.stack([np.tile(f32(inputs["k_norm"]), 2), np.tile(f32(inputs["q_norm"])[0], 2)], axis=1))
    perm = np.array([8 * k + 2 * jj + half for k in range(4) for half in range(2) for jj in range(4)])
    taug = np.zeros((34, 32), np.float32)
    taug[0:32] = f32(inputs["rel_bias_table"])[:, perm]
    taug[32] = -30000.0
    taug[33] = f32(inputs["sinks"])[0][perm]
    ident = np.eye(P, dtype=np.float32)
    oh_by_j = {}
    for j in (0, 1):
        oh_by_j[j] = _onehots(j)

    in_maps = []
    for c in range(8):
        b, j = c // 4, c % 4
        xs = np.zeros((NS, D), np.float32)
        if j == 0:
            xs[HALO - 16:HALO] = meta
        else:
            xs[0:HALO] = x_prompt[b, LM * j - HALO:LM * j]
        xs[META0:META0 + 16] = meta
        xs[SMP0:SMP0 + 64] = x_sample[c]
        um = np.ones((P, NS), np.float32)
        if j == 0:
            um[:, 0:HALO - 16] = 0.0
        ck = np.concatenate([f32(inputs["cache_meta_k"])[c], f32(inputs["cache_win_k"])[c]], axis=0)
        ckd = np.ascontiguousarray(np.stack([ck, ck], axis=2).reshape(144, 512))
        cv = np.ascontiguousarray(np.concatenate([f32(inputs["cache_meta_v"])[c], f32(inputs["cache_win_v"])[c]],
                                                 axis=0).reshape(144, 256))
        ohA, ohC, ohM = oh_by_j[0 if j == 0 else 1]
        in_maps.append({
            "xm": np.ascontiguousarray(x_prompt[b, LM * j:LM * (j + 1)]), "xs": xs, "umask": um,
            "sconv": f32(inputs["state_conv"])[0, c], "ckd": ckd, "cv": cv,
            "w_gate": w_gate, "w_up": w_up, "w_down": w_down, "w_pw1": w_pw1, "w_pw2": w_pw2, "wkd": wkd,
            "wvd": wvd, "w_q": w_q, "w_o": w_o, "gains": gains, "bpw1": bpw1, "wdw": wdw, "kqn": kqn,
            "taug": taug, "ohA": ohA, "ohC": ohC, "ohM": ohM, "ident": ident,
        })
    return in_maps


def assemble(R):
    y_prompt = np.stack([np.concatenate([R[4 * b + j]["y_main"] for j in range(4)], axis=0) for b in range(2)])
    y_sample = np.stack([R[c]["y_smp"] for c in range(8)])
    p_conv = np.stack([R[4 * b + 3]["cs_p"] for b in range(2)])[None]
    p_meta_k = np.stack([R[4 * b]["kmeta"] for b in range(2)]).reshape(2, 16, 4, 64)
    p_meta_v = np.stack([R[4 * b]["vmeta"] for b in range(2)]).reshape(2, 16, 4, 64)
    p_win_k = np.stack([R[4 * b + 3]["kwin"] for b in range(2)]).reshape(2, 128, 4, 64)
    p_win_v = np.stack([R[4 * b + 3]["vwin"] for b in range(2)]).reshape(2, 128, 4, 64)
    s_conv = np.stack([R[c]["cs_s"] for c in range(8)])[None]
    s_k = np.stack([R[c]["knew"] for c in range(8)]).reshape(8, 64, 4, 64)
    s_v = np.stack([R[c]["vnew"] for c in range(8)]).reshape(8, 64, 4, 64)
    outs = (y_prompt, y_sample, p_conv, p_meta_k, p_meta_v, p_win_k, p_win_v, s_conv, s_k, s_v)
    return tuple(np.ascontiguousarray(o, dtype=np.float32) for o in outs)


def run(inputs, NT):
    in_maps = prep_inputs(inputs, NT)
    if NT not in _CACHE:
        _CACHE[NT] = build_program(NT)
    nc = _CACHE[NT]
    res = run_bass_kernel_spmd(nc, in_maps, core_ids=list(range(8)))
    return assemble(res.results)


def kernel(**inputs):
    return run(inputs, 8)
```

```python
import contextlib
import math
import numpy as np
import concourse.bass as bass
import concourse.mybir as mybir
from concourse.bass_utils import run_bass_kernel_spmd

F32 = mybir.dt.float32
BF16 = mybir.dt.bfloat16
AF = mybir.ActivationFunctionType
ALU = mybir.AluOpType

P = 128
D = 2048
KD = 16
DFF = 5632
KF = 44
TN = 512
NS = 268
HALO = 188
META0 = 188
SMP0 = 204
EPS = 1e-6
SLOT = 8192
NRING = 3
SCRW = 19968
KW = 17 + 128 + TN
KWS = 17 + 128 + 64


class Clock:
    def __init__(self, sem, name):
        self.sem = sem; self.cnt = 0; self.name = name


class Eng:
    def __init__(self, e, clock, name, self_sync=True):
        self.e = e; self.clock = clock; self.name = name; self.seen = {}; self.self_sync = self_sync


class FW:
    def __init__(self, nc, stack, n_dma_sems=32):
        self.nc = nc

        def mk(name):
            return Clock(stack.enter_context(nc.semaphore(name)), name)
        self.pe = Eng(nc.tensor, mk("c_pe"), "pe", self_sync=False)
        self.act = Eng(nc.scalar, mk("c_act"), "act")
        self.dve = Eng(nc.vector, mk("c_dve"), "dve")
        self.pool = Eng(nc.gpsimd, mk("c_pool"), "pool")
        self.sp = Eng(nc.sync, mk("c_sp"), "sp")
        self.dma_clocks = {"sp": [mk(f"c_dmas{i}") for i in range(n_dma_sems // 2)],
                           "pool": [mk(f"c_dmap{i}") for i in range(n_dma_sems // 2)]}
        self.dma_rr = {"sp": 0, "pool": 0}
        self.last_w = {}
        self.readers = {}
        self.fence = []
        self.nwaits = 0
        self.nops = 0

    def _wait(self, eng, stamp):
        clock, val = stamp
        if clock is eng.clock and not eng.self_sync:
            return
        if eng.seen.get(clock.name, 0) >= val:
            return
        eng.e.wait_ge(clock.sem, val)
        eng.seen[clock.name] = val
        self.nwaits += 1

    def _deps(self, eng, reads, writes):
        for k in reads:
            s = self.last_w.get(k)
            if s is not None:
                self._wait(eng, s)
        for k in writes:
            s = self.last_w.get(k)
            if s is not None:
                self._wait(eng, s)
            rs = self.readers.get(k)
            if rs:
                for r in rs:
                    self._wait(eng, r)
            if s is None and rs is None and isinstance(k, tuple) and k[0] == "s":
                for f in self.fence:
                    self._wait(eng, f)

    def _commit(self, stamp, reads, writes):
        for k in writes:
            self.last_w[k] = stamp
            self.readers[k] = []
        for k in reads:
            lst = self.readers.setdefault(k, [])
            lst.append(stamp)
            if len(lst) > 48:
                best = {}
                for c, v in lst:
                    if c.name not in best or best[c.name][1] < v:
                        best[c.name] = (c, v)
                self.readers[k] = list(best.values())

    def op(self, eng, fn, reads=(), writes=()):
        self._deps(eng, reads, writes)
        ins = fn()
        self.nops += 1
        eng.clock.cnt += 1
        ins.then_inc(eng.clock.sem, 1)
        self._commit((eng.clock, eng.clock.cnt), reads, writes)

    def pe_group(self, fns, reads=(), writes=()):
        eng = self.pe
        self._deps(eng, reads, writes)
        ins = None
        for fn in fns:
            ins = fn()
        self.nops += len(fns)
        eng.clock.cnt += 1
        ins.then_inc(eng.clock.sem, 1)
        self._commit((eng.clock, eng.clock.cnt), reads, writes)

    def dma(self, q, out, in_, reads=(), writes=()):
        cl = self.dma_clocks[q.name]
        clock = cl[self.dma_rr[q.name]]
        self.dma_rr[q.name] = (self.dma_rr[q.name] + 1) % len(cl)
        if clock.cnt > 0:
            self._wait(q, (clock, clock.cnt))
        self._deps(q, reads, writes)
        ins = q.e.dma_start(out=out, in_=in_)
        clock.cnt += 16
        ins.then_inc(clock.sem, 16)
        self._commit((clock, clock.cnt), reads, writes)
        self.nops += 1

    def split_key(self, whole, subs):
        for sub in subs:
            if whole in self.last_w:
                self.last_w[sub] = self.last_w[whole]
            self.readers[sub] = list(self.readers.get(whole, []))

    def merge_key(self, whole, subs):
        allst = list(self.readers.get(whole, []))
        for sub in subs:
            s_ = self.last_w.pop(sub, None)
            if s_ is not None:
                allst.append(s_)
            allst += self.readers.pop(sub, [])
        best = {}
        for c, v in allst:
            if c.name not in best or best[c.name][1] < v:
                best[c.name] = (c, v)
        self.readers[whole] = list(best.values())

    def fence_scratch(self):
        best = {}
        for f in self.fence:
            best[f[0].name] = f
        dead = [k for k in list(self.last_w.keys()) + list(self.readers.keys())
                if isinstance(k, tuple) and k[0] == "s"]
        for k in set(dead):
            stamps = []
            s = self.last_w.pop(k, None)
            if s is not None:
                stamps.append(s)
            stamps += self.readers.pop(k, [])
            for c, v in stamps:
                if c.name not in best or best[c.name][1] < v:
                    best[c.name] = (c, v)
        self.fence = list(best.values())

    def wait_all(self, eng):
        for k, s in list(self.last_w.items()):
            self._wait(eng, s)
        for k, rs in list(self.readers.items()):
            for r in rs:
                self._wait(eng, r)


def build_program(NT):
    nc = bass.Bass("TRN2", target_bir_lowering=False)

    def din(name, shape):
        return nc.dram_tensor(name, list(shape), F32, kind="ExternalInput").ap()

    def dout(name, shape):
        return nc.dram_tensor(name, list(shape), F32, kind="ExternalOutput").ap()

    LM = NT * TN
    xm = din("xm", [LM, D]); xs = din("xs", [NS, D]); umask_d = din("umask", [P, NS])
    sconv_d = din("sconv", [30, D]); ckd_d = din("ckd", [144, 512]); cv_d = din("cv", [144, 256])
    w_gate = din("w_gate", [4, D, DFF]); w_up = din("w_up", [4, D, DFF]); w_down = din("w_down", [4, DFF, D])
    w_pw1 = din("w_pw1", [D, 2 * D]); w_pw2 = din("w_pw2", [D, D]); wkd_d = din("wkd", [D, 512])
    wvd_d = din("wvd", [D, 512]); w_q = din("w_q", [D, D]); w_o = din("w_o", [D, D])
    gains_d = din("gains", [P, 11 * 16]); bpw1_d = din("bpw1", [P, 32]); wdw_d = din("wdw", [P, 16 * 31])
    kqn_d = din("kqn", [P, 2]); taug_d = din("taug", [34, 32])
    ohA_d = din("ohA", [3, 34, 64 * 128]); ohC_d = din("ohC", [34, 64 * 64]); ohM_d = din("ohM", [3, 34, 64 * 17])
    ident_d = din("ident", [P, P])

    y_main = dout("y_main", [LM, D]); y_smp = dout("y_smp", [64, D])
    cs_p = dout("cs_p", [30, D]); cs_s = dout("cs_s", [30, D])
    kmeta_o = dout("kmeta", [16, 256]); vmeta_o = dout("vmeta", [16, 256])
    kwin_o = dout("kwin", [128, 256]); vwin_o = dout("vwin", [128, 256])
    knew_o = dout("knew", [64, 256]); vnew_o = dout("vnew", [64, 256])

    biasd = nc.dram_tensor("biasd", [7, P, 2048], F32).ap()

    wst = {}
    for f in range(4):
        wst[("gu", f)] = (22, 8192)
        wst[("dn", f)] = (16, 5632)
    wst["pw1"] = (8, 8192); wst["pw2"] = (8, 4096); wst["wk"] = (1, 8192); wst["wv"] = (1, 8192)
    wst["wq"] = (8, 4096); wst["wo"] = (8, 4096)
    wscr = {}
    for i, (k, (ns_, el)) in enumerate(wst.items()):
        wscr[k] = nc.dram_tensor(f"wscr{i}", [ns_, P, el], BF16).ap()

    def kpc(w):
        return w.rearrange("(k p) c -> p k c", p=P)

    def wsrc(key, n):
        if isinstance(key, tuple) and key[0] == "gu":
            f = key[1]
            return [(0, 4096, (16, 256), kpc(w_gate[f])[:, :, n * 256:(n + 1) * 256]),
                    (4096, 8192, (16, 256), kpc(w_up[f])[:, :, n * 256:(n + 1) * 256])]
        if isinstance(key, tuple) and key[0] == "dn":
            f = key[1]; fp, hh = n // 2, n % 2
            return [(0, 5632, (22, 256), kpc(w_down[f])[:, hh * 22:(hh + 1) * 22, fp * 256:(fp + 1) * 256])]
        if key == "pw1":
            return [(0, 4096, (16, 256), kpc(w_pw1)[:, :, n * 256:(n + 1) * 256]),
                    (4096, 8192, (16, 256), kpc(w_pw1)[:, :, D + n * 256:D + (n + 1) * 256])]
        if key in ("pw2", "wq", "wo"):
            w = {"pw2": w_pw2, "wq": w_q, "wo": w_o}[key]
            return [(0, 4096, (16, 256), kpc(w)[:, :, n * 256:(n + 1) * 256])]
        if key == "wk":
            return [(0, 8192, (16, 512), kpc(wkd_d))]
        if key == "wv":
            return [(0, 8192, (16, 512), kpc(wvd_d))]
        raise KeyError(key)

    with contextlib.ExitStack() as st:
        fw = FW(nc, st)
        pe, act, dve, pool, sp = fw.pe, fw.act, fw.dve, fw.pool, fw.sp

        def sb(name, shape, dt):
            return st.enter_context(nc.sbuf_tensor("s_" + name, list(shape), dt))

        xT = sb("xT", [P, KD, TN], F32)
        hT = sb("hT", [P, KD, TN], BF16)
        wring = sb("wring", [P, NRING, SLOT], BF16)
        scr = sb("scr", [P, SCRW], F32)
        hprev = sb("hprev", [P, KD, 128], BF16)
        KT = sb("KT", [P, 4, KW], BF16)
        KTs = sb("KTs", [P, 4, KWS], BF16)
        Vatt = sb("Vatt", [P, 10, 512], BF16)
        Vmeta = sb("Vmeta", [P, 512], BF16)
        VsA = sb("VsA", [P, 512], BF16); VsC = sb("VsC", [P, 512], BF16); VsM = sb("VsM", [P, 512], BF16)
        hist = sb("hist", [P, KD, 30], F32)
        ident = sb("ident", [P, P], F32)
        identb = sb("identb", [P, P], BF16)
        onesD = sb("onesD", [P, P], BF16)
        bd64 = sb("bd64", [P, P], BF16)
        onesB = sb("onesB", [P, P], BF16)
        gains = sb("gains", [P, 11, 16], F32)
        bpw1 = sb("bpw1", [P, 32], F32)
        wdw = sb("wdw", [P, 16, 31], F32)
        kqn = sb("kqn", [P, 2], F32)
        qg = sb("qg", [P, 1], F32)
        epsT = sb("epsT", [P, 1], F32)
        umask = sb("umask", [P, NS], F32)
        taug = sb("taug", [P, 32], F32)
        ps = [st.enter_context(nc.psum_tensor(f"ps{i}", [P, 512], F32)) for i in range(8)]

        def PSK(b):
            return ("ps", b)

        class Scr:
            def __init__(self):
                self.off = 0

            def reset(self):
                fw.fence_scratch(); self.off = 0

            def f32(self, n):
                a = self.off; self.off += n
                assert self.off <= SCRW, self.off
                return scr[:, a:a + n]

            def bf16(self, n):
                n2 = (n + 1) // 2
                a = self.off; self.off += n2
                assert self.off <= SCRW, self.off
                return scr[:, a:a + n2].bitcast(BF16)[:, 0:n]
        S = Scr()

        cq = sp
        fw.dma(cq, ident[:], ident_d, writes=["ident"])
        fw.dma(cq, gains[:], gains_d.rearrange("p (a b) -> p a b", b=16), writes=["gains"])
        fw.dma(cq, bpw1[:], bpw1_d, writes=["bpw1"])
        fw.dma(cq, wdw[:], wdw_d.rearrange("p (a b) -> p a b", b=31), writes=["wdw"])
        fw.dma(cq, kqn[:], kqn_d, writes=["kqn"])
        fw.dma(cq, umask[:], umask_d, writes=["umask"])
        fw.dma(cq, taug[0:34, :], taug_d, writes=["taug"])
        fw.op(dve, lambda: nc.vector.memset(onesD[:], 1.0 / D), writes=["onesD"])
        fw.op(dve, lambda: nc.vector.tensor_copy(out=identb[:], in_=ident[:]), reads=["ident"], writes=["identb"])
        fw.op(dve, lambda: nc.vector.memset(onesB[:], 1.0), writes=["onesB"])
        fw.op(dve, lambda: nc.vector.memset(bd64[:], 0.0), writes=["bd64"])
        fw.op(dve, lambda: nc.vector.memset(bd64[0:64, 0:64], 1.0 / 64), reads=["bd64"], writes=["bd64"])
        fw.op(dve, lambda: nc.vector.memset(bd64[64:128, 64:128], 1.0 / 64), reads=["bd64"], writes=["bd64"])
        fw.op(dve, lambda: nc.vector.memset(epsT[:], EPS), writes=["epsT"])
        fw.op(dve, lambda: nc.vector.tensor_scalar(out=qg[:], in0=kqn[:, 1:2], scalar1=0.125, scalar2=None,
                                                   op0=ALU.mult), reads=["kqn"], writes=["qg"])
        fw.op(dve, lambda: nc.vector.memset(KT[:], 0.0), writes=["KT"])
        fw.op(dve, lambda: nc.vector.memset(KTs[:], 0.0), writes=["KTs"])
        fw.op(dve, lambda: nc.vector.memset(Vmeta[:], 0.0), writes=["Vmeta"])
        fw.op(dve, lambda: nc.vector.memset(VsM[:], 0.0), writes=["VsM"])
        fw.op(dve, lambda: nc.vector.memset(hist[:], 0.0), writes=["hist"])
        fw.op(dve, lambda: nc.vector.memset(hprev[:], 0.0), writes=["hprev"])

        def build_bias():
            S.reset()
            ohb = S.f32(8192)
            bt = S.f32(2048)
            variants = [(0, ohA_d[0], 128), (1, ohA_d[1], 128), (2, ohA_d[2], 128), (3, ohC_d, 64),
                        (4, ohM_d[0], 17), (5, ohM_d[1], 17), (6, ohM_d[2], 17)]
            for v, src, nk in variants:
                fw.dma(sp, ohb[0:34, 0:64 * nk], src, writes=[("s", "ohb")])
                for b in range(4):
                    fns = []
                    for il in range(16):
                        i = b * 16 + il
                        fns.append(lambda i=i, il=il, b=b, nk=nk: nc.tensor.matmul(
                            ps[b][0:nk, il * 32:(il + 1) * 32], ohb[0:34, i * nk:(i + 1) * nk], taug[0:34, :],
                            start=True, stop=True))
                    fw.pe_group(fns, reads=[("s", "ohb"), "taug"], writes=[PSK(b)])
                    o_ap = bt[0:nk, :].rearrange("s (h i) -> s h i", i=64)[:, :, b * 16:(b + 1) * 16]
                    i_ap = ps[b][0:nk, :].rearrange("s (i h) -> s h i", h=32)
                    fw.op(dve, lambda o_ap=o_ap, i_ap=i_ap: nc.vector.tensor_copy(out=o_ap, in_=i_ap),
                          reads=[PSK(b)], writes=[("s", "bt")])
                fw.dma(sp, biasd[v, 0:nk, :], bt[0:nk, :], reads=[("s", "bt")], writes=[("biasd", v)])

        build_bias()

        ring = {"pos": 0}
        cur = {"t": 0}
        NCAST = 4

        def wload(key, n, tileS):
            s = ring["pos"] % NRING
            ring["pos"] += 1
            nsl, el = wst[key]
            pieces = wsrc(key, n)
            wkeys = [("w", s, 0), ("w", s, 1)]
            t = cur["t"]
            if t <= NCAST:
                for a, (lo, hi, (kk, cc), src) in enumerate(pieces):
                    dst = wring[:, s, lo:hi].rearrange("p (k c) -> p k c", c=cc)
                    fw.dma(pool, dst, src, writes=[("w", s, a)] if len(pieces) == 2 else wkeys)
                if t >= 1 and (n % NCAST) == (t - 1):
                    fw.dma(sp, wscr[key][n], wring[:, s, 0:el], reads=wkeys, writes=[("wscr", key, n)])
            else:
                fw.dma(sp, wring[:, s, 0:el], wscr[key][n], reads=[("wscr", key, n)], writes=wkeys)
            return s, wkeys

        def emit_norm(tile, c0, n, gidx, out_fn, out_keys):
            S.reset()
            sq = [S.bf16(TN) for _ in range(4)]
            rstd = S.f32(TN)
            for kc in range(KD):
                b = kc % 4
                if kc % 2 == 0:
                    fw.op(act, lambda kc=kc, b=b: nc.scalar.activation(out=sq[b][:, 0:n], in_=xT[:, kc, c0:c0 + n],
                                                                        func=AF.Square),
                          reads=[("xT", kc)], writes=[("s", "sq", b)])
                else:
                    fw.op(dve, lambda kc=kc, b=b: nc.vector.tensor_tensor(out=sq[b][:, 0:n], in0=xT[:, kc, c0:c0 + n],
                                                                          in1=xT[:, kc, c0:c0 + n], op=ALU.mult),
                          reads=[("xT", kc)], writes=[("s", "sq", b)])
                fw.pe_group([lambda kc=kc, b=b: nc.tensor.matmul(ps[6][:, 0:n], onesD[:], sq[b][:, 0:n],
                                                                  start=(kc == 0), stop=(kc == KD - 1))],
                            reads=[("s", "sq", b), "onesD"], writes=[PSK(6)])
            fw.op(act, lambda: nc.scalar.activation(out=rstd[:, 0:n], in_=ps[6][:, 0:n], func=AF.Ln,
                                                    bias=epsT[:, 0:1], scale=1.0),
                  reads=[PSK(6), "epsT"], writes=[("s", "rstd")])
            fw.op(act, lambda: nc.scalar.activation(out=rstd[:, 0:n], in_=rstd[:, 0:n], func=AF.Exp, scale=-0.5),
                  reads=[("s", "rstd")], writes=[("s", "rstd")])
            for kc in range(KD):
                fw.op(dve, lambda kc=kc: nc.vector.scalar_tensor_tensor(
                    out=out_fn(kc), in0=xT[:, kc, c0:c0 + n], scalar=gains[:, gidx, kc:kc + 1], in1=rstd[:, 0:n],
                    op0=ALU.mult, op1=ALU.mult),
                    reads=[("xT", kc), "gains", ("s", "rstd")], writes=[out_keys(kc)])

        def hT_out(n):
            return (lambda kc: hT[:, kc, 0:n]), (lambda kc: ("hT", kc))

        def emit_ffn(tileS, f, c0, n):
            of, ok = hT_out(n)
            emit_norm(tileS, c0, n, f, of, ok)
            S.reset()
            actT = S.bf16(KF * TN).rearrange("p (k t) -> p k t", t=TN)
            sg = [S.f32(TN) for _ in range(2)]
            hkeys = [("hT", kc) for kc in range(KD)]
            for g in range(22):
                s, wk = wload(("gu", f), g, tileS)
                wv = wring[:, s, :].rearrange("p (a k c) -> p a k c", a=2, k=16)
                for j in range(2):
                    fc = 2 * g + j
                    bg, bu = (0, 1) if fc % 2 == 0 else (2, 3)
                    for a, bank in ((0, bg), (1, bu)):
                        fns = [lambda kc=kc, a=a, bank=bank, j=j, wv=wv: nc.tensor.matmul(
                            ps[bank][:, 0:n], wv[:, a, kc, j * 128:(j + 1) * 128], hT[:, kc, 0:n],
                            start=(kc == 0), stop=(kc == KD - 1)) for kc in range(KD)]
                        fw.pe_group(fns, reads=wk + hkeys, writes=[PSK(bank)])
                    sgi = sg[fc % 2]
                    fw.op(act, lambda sgi=sgi, bg=bg: nc.scalar.activation(out=sgi[:, 0:n], in_=ps[bg][:, 0:n],
                                                                           func=AF.Silu),
                          reads=[PSK(bg)], writes=[("s", "sg", fc % 2)])
                    fw.op(dve, lambda sgi=sgi, bu=bu, fc=fc: nc.vector.tensor_tensor(
                        out=actT[:, fc, 0:n], in0=ps[bu][:, 0:n], in1=sgi[:, 0:n], op=ALU.mult),
                        reads=[PSK(bu), ("s", "sg", fc % 2)], writes=[("s", "actT", fc)])
            for fp in range(8):
                banks = (4, 5) if fp % 2 == 0 else (6, 7)
                for hh in range(2):
                    s, wk = wload(("dn", f), 2 * fp + hh, tileS)
                    wv = wring[:, s, 0:5632].rearrange("p (k c) -> p k c", c=256)
                    for j in range(2):
                        fns = [lambda kl=kl, hh=hh, j=j, wv=wv, bank=banks[j]: nc.tensor.matmul(
                            ps[bank][:, 0:n], wv[:, kl, j * 128:(j + 1) * 128], actT[:, hh * 22 + kl, 0:n],
                            start=(hh == 0 and kl == 0), stop=(hh == 1 and kl == 21)) for kl in range(22)]
                        fw.pe_group(fns, reads=wk + [("s", "actT", hh * 22 + kl) for kl in range(22)],
                                    writes=[PSK(banks[j])])
                for j in range(2):
                    fo = 2 * fp + j
                    fw.op(dve, lambda fo=fo, bank=banks[j]: nc.vector.scalar_tensor_tensor(
                        out=xT[:, fo, c0:c0 + n], in0=ps[bank][:, 0:n], scalar=0.5, in1=xT[:, fo, c0:c0 + n],
                        op0=ALU.mult, op1=ALU.add),
                        reads=[PSK(banks[j]), ("xT", fo)], writes=[("xT", fo)])

        uid = {"n": 0}

        def transpose_out(src_fn, ntok, dst, q, key):
            ost = S.f32(D)
            uid["n"] += 1
            oid = uid["n"]
            for g4 in range(4):
                bank = g4 % 2
                fns = [lambda kc=kc, bank=bank: nc.tensor.transpose(
                    ps[bank][0:ntok, (kc % 4) * 128:(kc % 4 + 1) * 128], src_fn(kc), ident[:])
                    for kc in range(4 * g4, 4 * g4 + 4)]
                fw.pe_group(fns, reads=[key(kc) for kc in range(4 * g4, 4 * g4 + 4)] + ["ident"],
                            writes=[PSK(bank)])
                if bank == 0:
                    fw.op(act, lambda g4=g4, bank=bank: nc.scalar.copy(out=ost[0:ntok, g4 * 512:(g4 + 1) * 512],
                                                                        in_=ps[bank][0:ntok, :]),
                          reads=[PSK(bank)], writes=[("s", "ost", oid, g4)])
                else:
                    fw.op(dve, lambda g4=g4, bank=bank: nc.vector.tensor_copy(out=ost[0:ntok, g4 * 512:(g4 + 1) * 512],
                                                                              in_=ps[bank][0:ntok, :]),
                          reads=[PSK(bank)], writes=[("s", "ost", oid, g4)])
            fw.dma(q, dst, ost[0:ntok, :], reads=[("s", "ost", oid, g4) for g4 in range(4)],
                   writes=[("out", oid)])

        def emit_input_load(tileS, t, mq):
            if tileS:
                blocks = [(0, 128), (128, 128), (256, 12)]
                src = xs
                r0 = 0
            else:
                blocks = [(0, 128), (128, 128), (256, 128), (384, 128)]
                src = xm
                r0 = (t - 1) * TN
            xin = S.f32(4 * D).rearrange("p (b c) -> p b c", c=D)
            for bi, (o, nt_) in enumerate(blocks):
                fw.dma(mq, xin[0:nt_, bi, :], src[r0 + o:r0 + o + nt_, :], writes=[("s", "xin", bi)])
            return xin, blocks

        def emit_input_xpose(xin, blocks):
            for kc in range(KD):
                bank = kc % 2
                fns = [lambda bi=bi, o=o, nt_=nt_, kc=kc, bank=bank: nc.tensor.transpose(
                    ps[bank][:, o:o + nt_], xin[0:nt_, bi, kc * 128:(kc + 1) * 128], ident[0:nt_, 0:nt_])
                    for bi, (o, nt_) in enumerate(blocks)]
                fw.pe_group(fns, reads=[("s", "xin", bi) for bi in range(len(blocks))] + ["ident"],
                            writes=[PSK(bank)])
                ncols = blocks[-1][0] + blocks[-1][1]
                fw.op(act if kc % 2 == 0 else dve,
                      (lambda kc=kc, bank=bank, ncols=ncols: nc.scalar.copy(out=xT[:, kc, 0:ncols], in_=ps[bank][:, 0:ncols]))
                      if kc % 2 == 0 else
                      (lambda kc=kc, bank=bank, ncols=ncols: nc.vector.tensor_copy(out=xT[:, kc, 0:ncols], in_=ps[bank][:, 0:ncols])),
                      reads=[PSK(bank)], writes=[("xT", kc)])

        def emit_conv(tileS, last, n, mq):
            of, ok = hT_out(n)
            emit_norm(tileS, 0, n, 4, of, ok)
            S.reset()
            if tileS:
                UW = 358; W = 328
                segs = [(0, 188, 30), (188, 16, 248), (204, 64, 294)]
                hsrc = 188
            else:
                UW = 542; W = 512
                segs = [(0, 512, 30)]
                hsrc = 512
            U = S.bf16(KD * UW).rearrange("p (k t) -> p k t", t=UW)
            Y = S.f32(KD * W).rearrange("p (k t) -> p k t", t=W)
            dg = [S.bf16(16 * P).rearrange("p (w c) -> p w c", c=P) for _ in range(2)]
            sig = [S.f32(TN) for _ in range(2)]
            ybf = [S.bf16(TN) for _ in range(2)]
            y2 = [S.bf16(TN) for _ in range(2)]
            m2 = S.f32(TN); rstd = S.f32(TN); nmr = S.f32(TN)
            tt = [S.f32(TN) for _ in range(2)]
            UK = lambda fc: ("s", "U", fc)
            if tileS:
                fw.op(dve, lambda: nc.vector.memset(U[:, :, 0:30], 0.0), writes=[UK(fc) for fc in range(KD)])
                fw.op(dve, lambda: nc.vector.memset(U[:, :, 218:248], 0.0),
                      reads=[UK(0)], writes=[("s", "Upad")])
                sct = S.f32(D)
                fw.dma(mq, sct[0:30, :], sconv_d, writes=[("s", "sct")])
                for fc in range(KD):
                    bank = 4 + fc % 2
                    fw.pe_group([lambda fc=fc, bank=bank: nc.tensor.transpose(
                        ps[bank][:, 0:30], sct[0:30, fc * 128:(fc + 1) * 128], ident[0:30, 0:30])],
                        reads=[("s", "sct"), "ident"], writes=[PSK(bank)])
                    fw.op(act, lambda fc=fc, bank=bank: nc.scalar.copy(out=U[:, fc, 264:294], in_=ps[bank][:, 0:30]),
                          reads=[PSK(bank), ("s", "Upad")], writes=[("s", "Uh", fc)])
            else:
                fw.op(act, lambda: nc.scalar.copy(out=U[:, :, 0:30], in_=hist[:]), reads=["hist"],
                      writes=[UK(fc) for fc in range(KD)])
            hkeys = [("hT", kc) for kc in range(KD)]
            for g in range(8):
                s, wk = wload("pw1", g, tileS)
                wv = wring[:, s, :].rearrange("p (a k c) -> p a k c", a=2, k=16)
                for j in range(2):
                    fc = 2 * g + j
                    ba, bg = (0, 1) if fc % 2 == 0 else (2, 3)
                    for a, bank in ((0, ba), (1, bg)):
                        fns = [lambda kc=kc, a=a, bank=bank, j=j, wv=wv: nc.tensor.matmul(
                            ps[bank][:, 0:n], wv[:, a, kc, j * 128:(j + 1) * 128], hT[:, kc, 0:n],
                            start=(kc == 0), stop=(kc == KD - 1)) for kc in range(KD)]
                        fw.pe_group(fns, reads=wk + hkeys, writes=[PSK(bank)])
                    sgi = sig[fc % 2]
                    fw.op(act, lambda sgi=sgi, bg=bg, fc=fc: nc.scalar.activation(
                        out=sgi[:, 0:n], in_=ps[bg][:, 0:n], func=AF.Sigmoid, bias=bpw1[:, 16 + fc:17 + fc], scale=1.0),
                        reads=[PSK(bg), "bpw1"], writes=[("s", "sig", fc % 2)])
                    for (c_, l_, u_) in segs:
                        fw.op(dve, lambda sgi=sgi, ba=ba, fc=fc, c_=c_, l_=l_, u_=u_: nc.vector.scalar_tensor_tensor(
                            out=U[:, fc, u_:u_ + l_], in0=ps[ba][:, c_:c_ + l_], scalar=bpw1[:, fc:fc + 1],
                            in1=sgi[:, c_:c_ + l_], op0=ALU.add, op1=ALU.mult),
                            reads=[PSK(ba), ("s", "sig", fc % 2), "bpw1", UK(fc), ("s", "Uh", fc), ("s", "Upad")],
                            writes=[UK(fc)])
                    if tileS:
                        fw.op(dve, lambda fc=fc: nc.vector.tensor_tensor(
                            out=U[:, fc, 30:218], in0=U[:, fc, 30:218], in1=umask[:, 0:188], op=ALU.mult),
                            reads=[UK(fc), "umask"], writes=[UK(fc)])
            fw.op(act, lambda: nc.scalar.copy(out=hist[:], in_=U[:, :, hsrc:hsrc + 30]),
                  reads=[UK(fc) for fc in range(KD)], writes=["hist"])
            if tileS:
                cst = S.f32(KD * 30).rearrange("p (k t) -> p k t", t=30)
                fw.op(act, lambda: nc.scalar.copy(out=cst[:], in_=U[:, :, 328:358]),
                      reads=[UK(fc) for fc in range(KD)], writes=[("s", "cst", kc_) for kc_ in range(KD)])
                transpose_out(lambda kc: cst[:, kc, :], 30, cs_s, mq, lambda kc: ("s", "cst", kc))
            YK = lambda fc: ("s", "Y", fc)
            for fc in range(KD):
                for hf, (w0, nw) in enumerate(((0, 16), (16, 15))):
                    fw.op(dve, lambda fc=fc, hf=hf, w0=w0, nw=nw: nc.vector.tensor_tensor(
                        out=dg[hf][:, 0:nw, :], in0=identb[:].unsqueeze(1).broadcast_to([P, nw, P]),
                        in1=wdw[:, fc, w0:w0 + nw].unsqueeze(2).broadcast_to([P, nw, P]), op=ALU.mult),
                        reads=["identb", "wdw"], writes=[("s", "dg", hf)])
                bank = 4 + fc % 2
                for hf, (w0, nw) in enumerate(((0, 16), (16, 15))):
                    fns = [lambda fc=fc, hf=hf, wl=wl, w0=w0, bank=bank: nc.tensor.matmul(
                        ps[bank][:, 0:W], dg[hf][:, wl, :], U[:, fc, w0 + wl:w0 + wl + W],
                        start=(w0 + wl == 0), stop=(w0 + wl == 30)) for wl in range(nw)]
                    fw.pe_group(fns, reads=[("s", "dg", hf), UK(fc), ("s", "Uh", fc), ("s", "Upad")], writes=[PSK(bank)])
                b2 = fc % 2
                fw.op(dve, lambda fc=fc, bank=bank: nc.vector.tensor_scalar(
                    out=Y[:, fc, :], in0=ps[bank][:, 0:W], scalar1=gains[:, 9, fc:fc + 1], scalar2=None, op0=ALU.add),
                    reads=[PSK(bank), "gains"], writes=[YK(fc)])
                fw.op(act, lambda fc=fc, b2=b2: nc.scalar.copy(out=ybf[b2][:, 0:W], in_=Y[:, fc, :]),
                      reads=[YK(fc)], writes=[("s", "ybf", b2)])
                fw.op(act, lambda fc=fc, b2=b2: nc.scalar.activation(out=y2[b2][:, 0:W], in_=Y[:, fc, :], func=AF.Square),
                      reads=[YK(fc)], writes=[("s", "y2", b2)])
                fw.pe_group([lambda fc=fc, b2=b2: nc.tensor.matmul(ps[6][:, 0:W], onesD[:], ybf[b2][:, 0:W],
                                                                   start=(fc == 0), stop=(fc == KD - 1))],
                            reads=[("s", "ybf", b2), "onesD"], writes=[PSK(6)])
                fw.pe_group([lambda fc=fc, b2=b2: nc.tensor.matmul(ps[7][:, 0:W], onesD[:], y2[b2][:, 0:W],
                                                                   start=(fc == 0), stop=(fc == KD - 1))],
                            reads=[("s", "y2", b2), "onesD"], writes=[PSK(7)])
            fw.op(dve, lambda: nc.vector.tensor_copy(out=m2[:, 0:W], in_=ps[6][:, 0:W]),
                  reads=[PSK(6)], writes=[("s", "m2")])
            fw.op(dve, lambda: nc.vector.tensor_tensor(out=nmr[:, 0:W], in0=m2[:, 0:W], in1=m2[:, 0:W], op=ALU.mult),
                  reads=[("s", "m2")], writes=[("s", "nmr")])
            fw.op(dve, lambda: nc.vector.tensor_tensor(out=rstd[:, 0:W], in0=ps[7][:, 0:W], in1=nmr[:, 0:W],
                                                       op=ALU.subtract),
                  reads=[PSK(7), ("s", "nmr")], writes=[("s", "rstd")])
            fw.op(act, lambda: nc.scalar.activation(out=rstd[:, 0:W], in_=rstd[:, 0:W], func=AF.Ln,
                                                    bias=epsT[:, 0:1], scale=1.0),
                  reads=[("s", "rstd"), "epsT"], writes=[("s", "rstd")])
            fw.op(act, lambda: nc.scalar.activation(out=rstd[:, 0:W], in_=rstd[:, 0:W], func=AF.Exp, scale=-0.5),
                  reads=[("s", "rstd")], writes=[("s", "rstd")])
            fw.op(dve, lambda: nc.vector.scalar_tensor_tensor(out=nmr[:, 0:W], in0=m2[:, 0:W], scalar=-1.0,
                                                              in1=rstd[:, 0:W], op0=ALU.mult, op1=ALU.mult),
                  reads=[("s", "m2"), ("s", "rstd"), ("s", "nmr")], writes=[("s", "nmr")])
            for fc in range(KD):
                t_ = tt[fc % 2]
                fw.op(dve, lambda fc=fc, t_=t_: nc.vector.tensor_tensor(out=t_[:, 0:W], in0=Y[:, fc, :],
                                                                        in1=rstd[:, 0:W], op=ALU.mult),
                      reads=[YK(fc), ("s", "rstd")], writes=[("s", "tt", fc % 2)])
                fw.op(dve, lambda fc=fc, t_=t_: nc.vector.tensor_tensor(out=t_[:, 0:W], in0=t_[:, 0:W],
                                                                        in1=nmr[:, 0:W], op=ALU.add),
                      reads=[("s", "tt", fc % 2), ("s", "nmr")], writes=[("s", "tt", fc % 2)])
                fw.op(act, lambda fc=fc, t_=t_: nc.scalar.activation(
                    out=hT[:, fc, 0:W], in_=t_[:, 0:W], func=AF.Silu, bias=gains[:, 8, fc:fc + 1],
                    scale=gains[:, 7, fc:fc + 1]),
                    reads=[("s", "tt", fc % 2), "gains"], writes=[("hT", fc)])
            if tileS:
                osegs = [(0, 188, 0), (188, 16, 218), (204, 64, 264)]
            else:
                osegs = [(0, 512, 0)]
            for g in range(8):
                s, wk = wload("pw2", g, tileS)
                wv = wring[:, s, 0:4096].rearrange("p (k c) -> p k c", c=256)
                for j in range(2):
                    fo = 2 * g + j
                    bank = 4 + fo % 2
                    fns = [lambda kc=kc, bank=bank, j=j, wv=wv: nc.tensor.matmul(
                        ps[bank][:, 0:W], wv[:, kc, j * 128:(j + 1) * 128], hT[:, kc, 0:W],
                        start=(kc == 0), stop=(kc == KD - 1)) for kc in range(KD)]
                    fw.pe_group(fns, reads=wk + hkeys, writes=[PSK(bank)])
                    for (c_, l_, o_) in osegs:
                        fw.op(dve, lambda fo=fo, bank=bank, c_=c_, l_=l_, o_=o_: nc.vector.scalar_tensor_tensor(
                            out=xT[:, fo, c_:c_ + l_], in0=ps[bank][:, o_:o_ + l_], scalar=gains[:, 10, fo:fo + 1],
                            in1=xT[:, fo, c_:c_ + l_], op0=ALU.add, op1=ALU.add),
                            reads=[PSK(bank), "gains", ("xT", fo)], writes=[("xT", fo)])

        def small_out(psk, src_ps, nrow, ncol, dst, q, dedup=False):
            stg = S.f32(256)
            uid["n"] += 1
            key = ("s", "stg", uid["n"])
            if dedup:
                o_ = stg[0:nrow, 0:256].rearrange("p (k d) -> p k d", d=64)
                i_ = src_ps.rearrange("p (k a d) -> p k a d", a=2, d=64)[:, :, 0, :]
            else:
                o_ = stg[0:nrow, 0:ncol]; i_ = src_ps
            fw.op(act, lambda: nc.scalar.copy(out=o_, in_=i_),
                  reads=[psk], writes=[key])
            fw.dma(q, dst, stg[0:nrow, 0:ncol], reads=[key], writes=[("out", uid["n"])])

        def emit_kv(tileS, last, n, mq):
            S.reset()
            hkv = S.bf16(KD * 640).rearrange("p (k t) -> p k t", t=640)
            HK = lambda kc: ("s", "hkv", kc)
            fw.op(act, lambda: nc.scalar.copy(out=hkv[:, :, 0:128], in_=hprev[:]), reads=["hprev"],
                  writes=[HK(kc) for kc in range(KD)])
            sq = [S.bf16(TN) for _ in range(4)]
            rstd = S.f32(TN)
            for kc in range(KD):
                b = kc % 4
                fw.op(act, lambda kc=kc, b=b: nc.scalar.activation(out=sq[b][:, 0:n], in_=xT[:, kc, 0:n], func=AF.Square),
                      reads=[("xT", kc)], writes=[("s", "sq", b)])
                fw.pe_group([lambda kc=kc, b=b: nc.tensor.matmul(ps[6][:, 0:n], onesD[:], sq[b][:, 0:n],
                                                                  start=(kc == 0), stop=(kc == KD - 1))],
                            reads=[("s", "sq", b), "onesD"], writes=[PSK(6)])
            fw.op(act, lambda: nc.scalar.activation(out=rstd[:, 0:n], in_=ps[6][:, 0:n], func=AF.Ln,
                                                    bias=epsT[:, 0:1], scale=1.0),
                  reads=[PSK(6), "epsT"], writes=[("s", "rstd")])
            fw.op(act, lambda: nc.scalar.activation(out=rstd[:, 0:n], in_=rstd[:, 0:n], func=AF.Exp, scale=-0.5),
                  reads=[("s", "rstd")], writes=[("s", "rstd")])
            for kc in range(KD):
                fw.op(dve, lambda kc=kc: nc.vector.scalar_tensor_tensor(
                    out=hkv[:, kc, 128:128 + n], in0=xT[:, kc, 0:n], scalar=gains[:, 5, kc:kc + 1], in1=rstd[:, 0:n],
                    op0=ALU.mult, op1=ALU.mult),
                    reads=[("xT", kc), "gains", ("s", "rstd"), HK(kc)], writes=[HK(kc)])
            hk_all = [HK(kc) for kc in range(KD)]
            s, wk = wload("wk", 0, tileS)
            wkv = wring[:, s, :].rearrange("p (k c) -> p k c", c=512)
            ksq = [S.bf16(TN) for _ in range(2)]
            krs = [S.f32(TN) for _ in range(2)]
            knf = [S.f32(TN) for _ in range(2)]
            for k in range(4):
                b2 = k % 2
                bk = 4 + b2
                fns = [lambda kc=kc, k=k, bk=bk: nc.tensor.matmul(
                    ps[bk][:, 0:n], wkv[:, kc, k * 128:(k + 1) * 128], hkv[:, kc, 128:128 + n],
                    start=(kc == 0), stop=(kc == KD - 1)) for kc in range(KD)]
                fw.pe_group(fns, reads=wk + hk_all, writes=[PSK(bk)])
                fw.op(act, lambda b2=b2, bk=bk: nc.scalar.activation(out=ksq[b2][:, 0:n], in_=ps[bk][:, 0:n], func=AF.Square),
                      reads=[PSK(bk)], writes=[("s", "ksq", b2)])
                fw.pe_group([lambda b2=b2: nc.tensor.matmul(ps[6 + b2][:, 0:n], bd64[:], ksq[b2][:, 0:n], start=True, stop=True)],
                            reads=[("s", "ksq", b2), "bd64"], writes=[PSK(6 + b2)])
                fw.op(act, lambda b2=b2: nc.scalar.activation(out=krs[b2][:, 0:n], in_=ps[6 + b2][:, 0:n], func=AF.Ln,
                                                              bias=epsT[:, 0:1], scale=1.0),
                      reads=[PSK(6 + b2), "epsT"], writes=[("s", "krs", b2)])
                fw.op(act, lambda b2=b2: nc.scalar.activation(out=krs[b2][:, 0:n], in_=krs[b2][:, 0:n], func=AF.Exp, scale=-0.5),
                      reads=[("s", "krs", b2)], writes=[("s", "krs", b2)])
                fw.op(dve, lambda b2=b2, bk=bk: nc.vector.scalar_tensor_tensor(
                    out=knf[b2][:, 0:n], in0=ps[bk][:, 0:n], scalar=kqn[:, 0:1], in1=krs[b2][:, 0:n],
                    op0=ALU.mult, op1=ALU.mult),
                    reads=[PSK(bk), "kqn", ("s", "krs", b2)], writes=[("s", "knf", b2)])
                KNF = ("s", "knf", b2)
                if tileS:
                    fw.op(act, lambda k=k, b2=b2: nc.scalar.copy(out=KT[:, k, 0:16], in_=knf[b2][:, 188:204]),
                          reads=[KNF, "KT"], writes=[("KT", k)])
                    fw.op(act, lambda k=k, b2=b2: nc.scalar.copy(out=KT[:, k, 17:145], in_=knf[b2][:, 60:188]),
                          reads=[KNF, ("KT", k)], writes=[("KT", k)])
                    fw.op(act, lambda k=k, b2=b2: nc.scalar.copy(out=KTs[:, k, 145:209], in_=knf[b2][:, 204:268]),
                          reads=[KNF, "KTs"], writes=[("KTs", k)])
                    for (c_, nt_, dst) in ((204, 64, knew_o), (188, 16, kmeta_o)):
                        bt_ = 2 + (k % 2)
                        fw.pe_group([lambda b2=b2, c_=c_, nt_=nt_, bt_=bt_: nc.tensor.transpose(
                            ps[bt_][0:nt_, 0:64], knf[b2][0:64, c_:c_ + nt_], ident[0:64, 0:64])],
                            reads=[KNF, "ident"], writes=[PSK(bt_)])
                        small_out(PSK(bt_), ps[bt_][0:nt_, 0:64], nt_, 64, dst[:, k * 64:(k + 1) * 64], mq)
                else:
                    fw.op(act, lambda k=k, b2=b2: nc.scalar.copy(out=KT[:, k, 145:145 + TN], in_=knf[b2][:, 0:TN]),
                          reads=[KNF, "KT", ("KT", k)], writes=[("KT", k)])
                    if last:
                        bt_ = 2 + (k % 2)
                        fw.pe_group([lambda b2=b2, bt_=bt_: nc.tensor.transpose(
                            ps[bt_][0:128, 0:64], knf[b2][0:64, 384:512], ident[0:64, 0:64])],
                            reads=[KNF, "ident"], writes=[PSK(bt_)])
                        small_out(PSK(bt_), ps[bt_][0:128, 0:64], 128, 64, kwin_o[:, k * 64:(k + 1) * 64], mq)
            s, wk = wload("wv", 0, tileS)
            wvv = wring[:, s, :].rearrange("p (k c) -> p k c", c=512)

            def vmm(bank, c_, m_):
                fns = [lambda kc=kc: nc.tensor.matmul(ps[bank][0:m_, 0:512], hkv[:, kc, c_:c_ + m_], wvv[:, kc, :],
                                                      start=(kc == 0), stop=(kc == KD - 1)) for kc in range(KD)]
                fw.pe_group(fns, reads=wk + hk_all, writes=[PSK(bank)])
            if tileS:
                vmm(0, 128 + META0, 16)
                fw.op(act, lambda: nc.scalar.copy(out=Vmeta[0:16, :], in_=ps[0][0:16, 0:512]),
                      reads=[PSK(0), "Vmeta"], writes=["Vmeta"])
                small_out(PSK(0), ps[0][0:16, 0:512], 16, 256, vmeta_o, mq, dedup=True)
                vmm(1, 128 + SMP0, 64)
                fw.op(act, lambda: nc.scalar.copy(out=VsC[0:64, :], in_=ps[1][0:64, 0:512]),
                      reads=[PSK(1)], writes=["VsC"])
                small_out(PSK(1), ps[1][0:64, 0:512], 64, 256, vnew_o, mq, dedup=True)
                cvt = S.f32(512)
                fw.dma(mq, cvt[:, 0:256], cv_d[16:144, :], writes=[("s", "cvw")])
                fw.dma(mq, cvt[0:16, 256:512], cv_d[0:16, :], writes=[("s", "cvm")])
                for a_ in range(2):
                    fw.op(dve, lambda a_=a_: nc.vector.tensor_copy(
                        out=VsA[:].rearrange("p (k a d) -> p k a d", a=2, d=64)[:, :, a_, :],
                        in_=cvt[:, 0:256].rearrange("p (k d) -> p k d", d=64)), reads=[("s", "cvw"), "VsA"], writes=["VsA"])
                    fw.op(dve, lambda a_=a_: nc.vector.tensor_copy(
                        out=VsM[0:16, :].rearrange("p (k a d) -> p k a d", a=2, d=64)[:, :, a_, :],
                        in_=cvt[0:16, 256:512].rearrange("p (k d) -> p k d", d=64)),
                        reads=[("s", "cvm"), "VsM"], writes=["VsM"])
                ckw = S.f32(512); ckm = S.f32(512)
                fw.dma(mq, ckw[:, :], ckd_d[16:144, :], writes=[("s", "ckw")])
                fw.dma(mq, ckm[0:16, :], ckd_d[0:16, :], writes=[("s", "ckm")])
                for k in range(4):
                    bt_ = 2 + (k % 2)
                    fw.pe_group([lambda k=k, bt_=bt_: nc.tensor.transpose(ps[bt_][:, 0:128], ckw[:, k * 128:(k + 1) * 128], ident[:]),
                                 lambda k=k, bt_=bt_: nc.tensor.transpose(ps[bt_][:, 128:144], ckm[0:16, k * 128:(k + 1) * 128],
                                                                          ident[0:16, 0:16])],
                                reads=[("s", "ckw"), ("s", "ckm"), "ident"], writes=[PSK(bt_)])
                    fw.op(act, lambda k=k, bt_=bt_: nc.scalar.copy(out=KTs[:, k, 17:145], in_=ps[bt_][:, 0:128]),
                          reads=[PSK(bt_), "KTs", ("KTs", k)], writes=[("KTs", k)])
                    fw.op(act, lambda k=k, bt_=bt_: nc.scalar.copy(out=KTs[:, k, 0:16], in_=ps[bt_][:, 128:144]),
                          reads=[PSK(bt_), ("KTs", k)], writes=[("KTs", k)])
                fw.op(act, lambda: nc.scalar.copy(out=hprev[:], in_=hkv[:, :, 128 + 60:128 + 188]),
                      reads=hk_all, writes=["hprev"])
            else:
                for m in range(10):
                    bank = m % 2
                    m_ = 128 if m < 9 else 64
                    vmm(bank, 64 * m, m_)
                    fw.op(act if m % 2 == 0 else dve,
                          (lambda m=m, bank=bank, m_=m_: nc.scalar.copy(out=Vatt[0:m_, m, :], in_=ps[bank][0:m_, 0:512]))
                          if m % 2 == 0 else
                          (lambda m=m, bank=bank, m_=m_: nc.vector.tensor_copy(out=Vatt[0:m_, m, :], in_=ps[bank][0:m_, 0:512])),
                          reads=[PSK(bank)], writes=[("Vatt", m)])
                    if last and m == 8:
                        small_out(PSK(bank), ps[bank][0:128, 0:512], 128, 256, vwin_o, mq, dedup=True)
                fw.op(act, lambda: nc.scalar.copy(out=hprev[:], in_=hkv[:, :, 512:640]), reads=hk_all, writes=["hprev"])

        def emit_attn(tileS, first, c0, n, mq):
            of, ok = hT_out(n)
            emit_norm(tileS, c0, n, 6, of, ok)
            S.reset()
            qTe = S.bf16(KD * TN).rearrange("p (k t) -> p k t", t=TN)
            qTo = S.bf16(KD * TN).rearrange("p (k t) -> p k t", t=TN)
            qTh = (qTe, qTo)
            QK = lambda qc: ("s", "qT", qc)
            fw.op(dve, lambda: nc.vector.memset(qTe[64:128, :, :], 0.0), writes=[("s", "qz", 0)])
            fw.op(dve, lambda: nc.vector.memset(qTo[0:64, :, :], 0.0), writes=[("s", "qz", 1)])
            NQB = 2 if first else 4
            qsq = [S.bf16(TN) for _ in range(NQB)]
            qrs = [S.f32(TN) for _ in range(NQB)]
            hkeys = [("hT", kc) for kc in range(KD)]
            for g in range(8):
                s, wk = wload("wq", g, tileS)
                wv = wring[:, s, 0:4096].rearrange("p (k c) -> p k c", c=256)
                for j in range(2):
                    qc = 2 * g + j
                    b2 = qc % NQB
                    bank = (4, 5, 0, 1)[b2]
                    sbank = (6, 7, 2, 3)[b2]
                    fns = [lambda kc=kc, bank=bank, j=j, wv=wv: nc.tensor.matmul(
                        ps[bank][:, 0:n], wv[:, kc, j * 128:(j + 1) * 128], hT[:, kc, 0:n],
                        start=(kc == 0), stop=(kc == KD - 1)) for kc in range(KD)]
                    fw.pe_group(fns, reads=wk + hkeys, writes=[PSK(bank)])
                    fw.op(act, lambda b2=b2, bank=bank: nc.scalar.activation(out=qsq[b2][:, 0:n], in_=ps[bank][:, 0:n], func=AF.Square),
                          reads=[PSK(bank)], writes=[("s", "qsq", b2)])
                    fw.pe_group([lambda b2=b2, sbank=sbank: nc.tensor.matmul(ps[sbank][:, 0:n], bd64[:], qsq[b2][:, 0:n], start=True, stop=True)],
                                reads=[("s", "qsq", b2), "bd64"], writes=[PSK(sbank)])
                    fw.op(act, lambda b2=b2, sbank=sbank: nc.scalar.activation(out=qrs[b2][:, 0:n], in_=ps[sbank][:, 0:n], func=AF.Ln,
                                                                  bias=epsT[:, 0:1], scale=1.0),
                          reads=[PSK(sbank), "epsT"], writes=[("s", "qrs", b2)])
                    fw.op(act, lambda b2=b2: nc.scalar.activation(out=qrs[b2][:, 0:n], in_=qrs[b2][:, 0:n], func=AF.Exp, scale=-0.5),
                          reads=[("s", "qrs", b2)], writes=[("s", "qrs", b2)])
                    for half in range(2):
                        hs = slice(half * 64, (half + 1) * 64)
                        fw.op(dve, lambda b2=b2, bank=bank, qc=qc, half=half, hs=hs: nc.vector.scalar_tensor_tensor(
                            out=qTh[half][hs, qc, 0:n], in0=ps[bank][hs, 0:n], scalar=qg[hs, 0:1], in1=qrs[b2][hs, 0:n],
                            op0=ALU.mult, op1=ALU.mult),
                            reads=[PSK(bank), "qg", ("s", "qrs", b2), ("s", "qz", 0), ("s", "qz", 1)],
                            writes=[("s", "qT", qc, half)])
            bias_gen = [[S.f32(512) for _ in range(3)] for _ in range(2)]
            bias_sp = [S.f32(512) for _ in range(4)] if first else None
            Tt = [[S.f32(512) for _ in range(3)] for _ in range(2)]
            PT = [[S.bf16(512) for _ in range(3)] for _ in range(2)]
            rd = S.f32(512)
            nch = n // 64
            if tileS:
                Kt = KTs; colA = 17; colC = 145
                ktk = lambda k: ("KTs", k)
            else:
                Kt = KT
                ktk = lambda k: ("KT", k)
            pending = [None]

            def attn_iter(k, lc, bs, st_):
                if tileS:
                    cA, cC = 17, 145
                    vkeys = ["VsA", "VsC", "VsM"]
                    VA = lambda: VsA[0:128, k * 128:(k + 1) * 128]
                    VC = lambda: VsC[0:64, k * 128:(k + 1) * 128]
                    VM = lambda: VsM[0:17, k * 128:(k + 1) * 128]
                else:
                    cA, cC = 17 + 64 * lc, 145 + 64 * lc
                    vkeys = [("Vatt", lc), ("Vatt", lc + 2), "Vmeta"]
                    VA = lambda: Vatt[0:128, lc, k * 128:(k + 1) * 128]
                    VC = lambda: Vatt[0:64, lc + 2, k * 128:(k + 1) * 128]
                    VM = lambda: Vmeta[0:17, k * 128:(k + 1) * 128]
                blocks = [(128, cA, VA), (64, cC, VC), (17, 0, VM)]
                if first and lc < 2:
                    btile = [bias_sp[lc], bias_gen[bs][1], bias_sp[2 + lc]]
                    bkeys = [("s", "bsp", lc), ("s", "bg", bs, 1), ("s", "bsp", 2 + lc)]
                else:
                    btile = bias_gen[bs]
                    bkeys = [("s", "bg", bs, bi) for bi in range(3)]
                qkeys = [("s", "qT", 4 * k + jj, hf_) for jj in range(4) for hf_ in range(2)]
                for bi, (nk, cc, _) in enumerate(blocks):
                    bk = 3 * st_ + bi
                    fns = [lambda half=half, bk=bk, nk=nk, cc=cc: nc.tensor.matmul(
                        ps[bk][0:nk, half * 256:(half + 1) * 256].rearrange("p (a b) -> p a b", b=64),
                        Kt[:, k, cc:cc + nk], qTh[half][:, 4 * k:4 * k + 4, lc * 64:(lc + 1) * 64],
                        start=True, stop=True) for half in range(2)]
                    fw.pe_group(fns, reads=[ktk(k)] + qkeys, writes=[PSK(bk)])
                    fw.op(dve, lambda bi=bi, nk=nk, bk=bk: nc.vector.tensor_tensor(
                        out=Tt[st_][bi][0:nk, :], in0=ps[bk][0:nk, :], in1=btile[bi][0:nk, :], op=ALU.add),
                        reads=[PSK(bk), bkeys[bi]], writes=[("s", "Tt", st_, bi)])
                    fw.op(act, lambda bi=bi, nk=nk: nc.scalar.activation(
                        out=PT[st_][bi][0:nk, :], in_=Tt[st_][bi][0:nk, :], func=AF.Exp),
                        reads=[("s", "Tt", st_, bi)], writes=[("s", "PT", st_, bi)])

                def phase2():
                    ptk = [("s", "PT", st_, bi) for bi in range(3)]
                    PTs = PT[st_]
                    bden, bo = 6, 7
                    fns = [lambda bi=bi, nk=nk: nc.tensor.matmul(
                        ps[bden][:, :], onesB[0:nk, :], PTs[bi][0:nk, :], start=(bi == 0), stop=(bi == 2))
                        for bi, (nk, _, _) in enumerate(blocks)]
                    fw.pe_group(fns, reads=ptk + ["onesB"], writes=[PSK(bden)])
                    fns = [lambda bi=bi, nk=nk, vf=vf: nc.tensor.matmul(
                        ps[bo][:, :], vf(), PTs[bi][0:nk, :], start=(bi == 0), stop=(bi == 2))
                        for bi, (nk, _, vf) in enumerate(blocks)]
                    fw.pe_group(fns, reads=ptk + vkeys, writes=[PSK(bo)])
                    fw.op(act, lambda: nc.scalar.activation(out=rd[:, :], in_=ps[bden][:, :], func=AF.Ln),
                          reads=[PSK(bden)], writes=[("s", "rd")])
                    fw.op(act, lambda: nc.scalar.activation(out=rd[:, :], in_=rd[:, :], func=AF.Exp, scale=-1.0),
                          reads=[("s", "rd")], writes=[("s", "rd")])
                    for half in range(2):
                        fw.op(dve, lambda half=half: nc.vector.tensor_tensor(
                            out=hT[half * 64:(half + 1) * 64, 4 * k:4 * k + 4, lc * 64:(lc + 1) * 64],
                            in0=ps[bo][half * 64:(half + 1) * 64, half * 256:(half + 1) * 256].rearrange("p (a b) -> p a b", b=64),
                            in1=rd[half * 64:(half + 1) * 64, half * 256:(half + 1) * 256].rearrange("p (a b) -> p a b", b=64),
                            op=ALU.mult),
                            reads=[PSK(bo), ("s", "rd")] + [("hT", 4 * k + jj) for jj in range(4)],
                            writes=[("hT", 4 * k + jj) for jj in range(4)])
                return phase2

            cnt = 0
            for k in range(4):
                bs = k % 2
                for bi, v in enumerate((0, 3, 4)):
                    nk = (128, 64, 17)[bi]
                    fw.dma(mq, bias_gen[bs][bi][0:nk, :], biasd[v, 0:nk, k * 512:(k + 1) * 512],
                           reads=[("biasd", v)], writes=[("s", "bg", bs, bi)])
                if first:
                    for bi, v in enumerate((1, 2, 5, 6)):
                        nk = 128 if bi < 2 else 17
                        fw.dma(mq, bias_sp[bi][0:nk, :], biasd[v, 0:nk, k * 512:(k + 1) * 512],
                               reads=[("biasd", v)], writes=[("s", "bsp", bi)])
                for lc in range(nch):
                    ph2 = attn_iter(k, lc, bs, cnt % 2)
                    cnt += 1
                    if pending[0] is not None:
                        pending[0]()
                    pending[0] = ph2
            if pending[0] is not None:
                pending[0]()
            if not tileS:
                fw.op(act, lambda: nc.scalar.copy(out=KT[:, :, 17:145], in_=KT[:, :, 145 + 384:145 + 512]),
                      reads=[("KT", k) for k in range(4)], writes=[("KT", k) for k in range(4)])
            for g in range(8):
                s, wk = wload("wo", g, tileS)
                wv = wring[:, s, 0:4096].rearrange("p (k c) -> p k c", c=256)
                for j in range(2):
                    fo = 2 * g + j
                    bank = 4 + fo % 2
                    fns = [lambda kc=kc, bank=bank, j=j, wv=wv: nc.tensor.matmul(
                        ps[bank][:, 0:n], wv[:, kc, j * 128:(j + 1) * 128], hT[:, kc, 0:n],
                        start=(kc == 0), stop=(kc == KD - 1)) for kc in range(KD)]
                    fw.pe_group(fns, reads=wk + hkeys, writes=[PSK(bank)])
                    fw.op(dve, lambda fo=fo, bank=bank: nc.vector.tensor_tensor(
                        out=xT[:, fo, c0:c0 + n], in0=ps[bank][:, 0:n], in1=xT[:, fo, c0:c0 + n], op=ALU.add),
                        reads=[PSK(bank), ("xT", fo)], writes=[("xT", fo)])

        def emit_output(tileS, t, mq):
            if tileS:
                transpose_out(lambda kc: xT[:, kc, SMP0:SMP0 + 64], 64, y_smp, mq, lambda kc: ("xT", kc))
            else:
                for b in range(4):
                    r0 = (t - 1) * TN + b * 128
                    transpose_out(lambda kc, b=b: xT[:, kc, b * 128:(b + 1) * 128], 128, y_main[r0:r0 + 128, :], mq,
                                  lambda kc: ("xT", kc))

        import os
        KSTOP = int(os.environ.get("KSTOP", "999"))
        stage_no = [0]

        def go():
            stage_no[0] += 1
            return stage_no[0] <= KSTOP
        pend_in = [None]
        for t in range(NT + 1):
            tileS = (t == 0)
            last = (t == NT)
            cur["t"] = t
            mq = sp if t <= NCAST else pool
            n0 = NS if tileS else TN
            c1, n1 = (SMP0, 64) if tileS else (0, TN)
            if pend_in[0] is None:
                S.reset()
                pend_in[0] = emit_input_load(tileS, t, mq)
            emit_input_xpose(*pend_in[0])
            pend_in[0] = None
            if go(): emit_ffn(tileS, 0, 0, n0)
            if go(): emit_conv(tileS, last, n0, mq)
            if go(): emit_ffn(tileS, 1, 0, n0)
            if go(): emit_kv(tileS, last, n0, mq)
            if go(): emit_ffn(tileS, 2, c1, n1)
            if go(): emit_attn(tileS, t == 1, c1, n1, mq)
            if go(): emit_ffn(tileS, 3, c1, n1)
            S.reset()
            if t < NT:
                pend_in[0] = emit_input_load(False, t + 1, mq)
            emit_output(tileS, t, mq)
        S.reset()
        transpose_out(lambda kc: hist[:, kc, :], 30, cs_p, pool, lambda kc: "hist")
        for e in (sp, pool, act, dve, pe):
            fw.wait_all(e)
    return nc


def _buckets(rel):
    import jax
    import jax.numpy as jnp
    cpu = jax.devices("cpu")[0]
    with jax.default_device(cpu):
        rel = jnp.asarray(np.asarray(rel, np.int32))
        nb = 16
        max_exact = 8
        n = jnp.abs(rel)
        large = max_exact + (jnp.log(jnp.maximum(n, 1).astype(jnp.float32) / max_exact)
                             / math.log(128 / max_exact) * (nb - max_exact)).astype(jnp.int32)
        large = jnp.minimum(large, nb - 1)
        bucket = jnp.where(rel > 0, nb, 0) + jnp.where(n < max_exact, n, large)
        return np.asarray(bucket)


def _onehots(j):
    i = np.arange(64)
    relA = np.arange(128)[None, :] - 128 - i[:, None]
    bA = _buckets(relA)
    relC = np.arange(64)[None, :] - i[:, None]
    bC = _buckets(relC)

    def oh(b, nk):
        o = np.zeros((34, 64, nk), np.float32)
        ii, ss = np.meshgrid(np.arange(64), np.arange(nk), indexing="ij")
        o[b, ii, ss] = 1.0
        return o
    A_gen = bA.copy()
    A_c0 = bA.copy(); A_c1 = bA.copy()
    if j == 0:
        A_c0[:, :] = 32
        A_c1[:, 0:64] = 32
    ohA = np.stack([oh(A_gen, 128), oh(A_c0, 128), oh(A_c1, 128)]).reshape(3, 34, 64 * 128)
    ohC = oh(bC, 64).reshape(34, 64 * 64)

    def mb(cg):
        qpos = 16 + cg * 64 + i
        rel = np.arange(16)[None, :] - qpos[:, None]
        b = _buckets(rel)
        return np.concatenate([b, np.full((64, 1), 33, b.dtype)], axis=1)
    M_gen = mb(2)
    M_c0 = mb(0) if j == 0 else M_gen
    M_c1 = mb(1) if j == 0 else M_gen
    ohM = np.stack([oh(M_gen, 17), oh(M_c0, 17), oh(M_c1, 17)]).reshape(3, 34, 64 * 17)
    return ohA, ohC, ohM


def _pl(v):
    return np.ascontiguousarray(np.asarray(v, np.float32).reshape(16, 128).T)


_CACHE = {}


def prep_inputs(inputs, NT):
    f32 = lambda a: np.ascontiguousarray(np.asarray(a, dtype=np.float32))
    x_prompt = f32(inputs["x_prompt"]); x_sample = f32(inputs["x_sample"])
    LM = NT * TN
    assert x_prompt.shape[1] == 4 * LM
    meta = f32(inputs["meta_tokens"])
    w_gate = f32(inputs["ffn_w_gate"]).reshape(4, D, DFF)
    w_up = f32(inputs["ffn_w_up"]).reshape(4, D, DFF)
    w_down = f32(inputs["ffn_w_down"]).reshape(4, DFF, D)
    w_pw1 = f32(inputs["conv_w_pw1"])[0]; w_pw2 = f32(inputs["conv_w_pw2"])[0]
    w_k = f32(inputs["w_k"]); w_v = f32(inputs["w_v"])
    w_q = f32(inputs["w_q"])[0]; w_o = f32(inputs["w_o"])[0]
    wkd = np.ascontiguousarray(np.stack([w_k.reshape(D, 4, 64)] * 2, axis=2).reshape(D, 512))
    wvd = np.ascontiguousarray(np.stack([w_v.reshape(D, 4, 64)] * 2, axis=2).reshape(D, 512))
    fn = f32(inputs["ffn_norm"]).reshape(4, D)
    vecs = [fn[0], fn[1], fn[2], fn[3], inputs["conv_norm"][0], inputs["kv_norm"], inputs["attn_norm"][0],
            inputs["conv_ln_g"][0], inputs["conv_ln_b"][0], inputs["conv_b_dw"][0], inputs["conv_b_pw2"][0]]
    gains = np.ascontiguousarray(np.stack([_pl(v) for v in vecs], axis=1).reshape(P, 11 * 16))
    bpw1 = np.ascontiguousarray(f32(inputs["conv_b_pw1"])[0].reshape(32, 128).T)
    wdw = np.ascontiguousarray(f32(inputs["conv_w_dw"])[0].reshape(31, 16, 128).transpose(2, 1, 0).reshape(P, 16 * 31))
    kqn = np.ascontiguousarray(np.stack([np.tile(f32(inputs["k_norm"]), 2), np.tile(f32(inputs["q_norm"])[0], 2)], axis=1))
    perm = np.array([8 * k + 2 * jj + half for k in range(4) for half in range(2) for jj in range(4)])
    taug = np.zeros((34, 32), np.float32)
    taug[0:32] = f32(inputs["rel_bias_table"])[:, perm]
    taug[32] = -30000.0
    taug[33] = f32(inputs["sinks"])[0][perm]
    ident = np.eye(P, dtype=np.float32)
    oh_by_j = {}
    for j in (0, 1):
        oh_by_j[j] = _onehots(j)

    in_maps = []
    for c in range(8):
        b, j = c // 4, c % 4
        xs = np.zeros((NS, D), np.float32)
        if j == 0:
            xs[HALO - 16:HALO] = meta
        else:
            xs[0:HALO] = x_prompt[b, LM * j - HALO:LM * j]
        xs[META0:META0 + 16] = meta
        xs[SMP0:SMP0 + 64] = x_sample[c]
        um = np.ones((P, NS), np.float32)
        if j == 0:
            um[:, 0:HALO - 16] = 0.0
        ck = np.concatenate([f32(inputs["cache_meta_k"])[c], f32(inputs["cache_win_k"])[c]], axis=0)
        ckd = np.ascontiguousarray(np.stack([ck, ck], axis=2).reshape(144, 512))
        cv = np.ascontiguousarray(np.concatenate([f32(inputs["cache_meta_v"])[c], f32(inputs["cache_win_v"])[c]],
                                                 axis=0).reshape(144, 256))
        ohA, ohC, ohM = oh_by_j[0 if j == 0 else 1]
        in_maps.append({
            "xm": np.ascontiguousarray(x_prompt[b, LM * j:LM * (j + 1)]), "xs": xs, "umask": um,
            "sconv": f32(inputs["state_conv"])[0, c], "ckd": ckd, "cv": cv,
            "w_gate": w_gate, "w_up": w_up, "w_down": w_down, "w_pw1": w_pw1, "w_pw2": w_pw2, "wkd": wkd,
            "wvd": wvd, "w_q": w_q, "w_o": w_o, "gains": gains, "bpw1": bpw1, "wdw": wdw, "kqn": kqn,
            "taug": taug, "ohA": ohA, "ohC": ohC, "ohM": ohM, "ident": ident,
        })
    return in_maps


def assemble(R):
    y_prompt = np.stack([np.concatenate([R[4 * b + j]["y_main"] for j in range(4)], axis=0) for b in range(2)])
    y_sample = np.stack([R[c]["y_smp"] for c in range(8)])
    p_conv = np.stack([R[4 * b + 3]["cs_p"] for b in range(2)])[None]
    p_meta_k = np.stack([R[4 * b]["kmeta"] for b in range(2)]).reshape(2, 16, 4, 64)
    p_meta_v = np.stack([R[4 * b]["vmeta"] for b in range(2)]).reshape(2, 16, 4, 64)
    p_win_k = np.stack([R[4 * b + 3]["kwin"] for b in range(2)]).reshape(2, 128, 4, 64)
    p_win_v = np.stack([R[4 * b + 3]["vwin"] for b in range(2)]).reshape(2, 128, 4, 64)
    s_conv = np.stack([R[c]["cs_s"] for c in range(8)])[None]
    s_k = np.stack([R[c]["knew"] for c in range(8)]).reshape(8, 64, 4, 64)
    s_v = np.stack([R[c]["vnew"] for c in range(8)]).reshape(8, 64, 4, 64)
    outs = (y_prompt, y_sample, p_conv, p_meta_k, p_meta_v, p_win_k, p_win_v, s_conv, s_k, s_v)
    return tuple(np.ascontiguousarray(o, dtype=np.float32) for o in outs)


def run(inputs, NT):
    in_maps = prep_inputs(inputs, NT)
    if NT not in _CACHE:
        _CACHE[NT] = build_program(NT)
    nc = _CACHE[NT]
    res = run_bass_kernel_spmd(nc, in_maps, core_ids=list(range(8)))
    return assemble(res.results)


def kernel(**inputs):
    return run(inputs, 8)
```
